# Optimizing a Trainium2 kernel written in Bass

```python
import jax, jax.numpy as jnp
from jax import lax
import numpy as np

D_MODEL = 2048
BATCH = 8
SEQ = 2048
DEPTH = 4

NORM_EPS = 1e-5
ROPE_THETA = 500000.0
ROPE_FRACTION = 4
BAND_BLOCK = 128

A_DIM = 128
A_HEADS = D_MODEL // 2 // A_DIM
A_BRANCHES = ((128, 1), (512, 4), (2048, 16))
A_WIDTH = A_HEADS * A_DIM
B_KDIM = 128
B_VDIM = 128
B_HEADS = D_MODEL // 2 // B_VDIM
B_KWIDTH = B_HEADS * B_KDIM
B_WIDTH = B_HEADS * B_VDIM
B_CHUNK = 64
EVEN_SPLITS = (A_WIDTH, A_WIDTH, A_WIDTH, B_KWIDTH, B_KWIDTH, B_WIDTH, B_WIDTH)
EVEN_IN = sum(EVEN_SPLITS)
EVEN_MIX = A_WIDTH + B_WIDTH

C_DIM = 64
C_Q_HEADS = D_MODEL // C_DIM
C_KV_HEADS = C_Q_HEADS // 8
C_GROUP = C_Q_HEADS // C_KV_HEADS
C_WINDOW = 128
C_QKV = (C_Q_HEADS + 2 * C_KV_HEADS) * C_DIM

D_FF = 4 * D_MODEL
N_EVEN = (DEPTH + 1) // 2
N_ODD = DEPTH // 2

kernel_name = "hybrid_dilated_hgrn2_swa_sink_trunk"


def rmsnorm(x, g):
    xf = x.astype(jnp.float32)
    y = xf * lax.rsqrt(jnp.mean(jnp.square(xf), axis=-1, keepdims=True) + NORM_EPS)
    return (y * g.astype(jnp.float32)).astype(x.dtype)


def rope_tables(seq, head_dim):
    rot = head_dim // ROPE_FRACTION
    inv_freq = 1.0 / (ROPE_THETA ** (jnp.arange(0, rot, 2, dtype=jnp.float32) / rot))
    ang = jnp.arange(seq, dtype=jnp.float32)[:, None] * inv_freq[None, :]
    return jnp.cos(ang), jnp.sin(ang)


def apply_partial_rope(x, cos, sin):
    half = cos.shape[-1]
    xf = x.astype(jnp.float32)
    x1, x2, rest = xf[..., :half], xf[..., half:2 * half], xf[..., 2 * half:]
    out = jnp.concatenate([x1 * cos - x2 * sin, x2 * cos + x1 * sin, rest], axis=-1)
    return out.astype(x.dtype)


def banded_attention(q, k, v, max_dist, sink=None):
    L, D = q.shape[-2], q.shape[-1]
    nb = -(-L // BAND_BLOCK)
    pad_r = nb * BAND_BLOCK - L

    def pad_seq(t, left):
        return jnp.pad(t, [(0, 0)] * (t.ndim - 2) + [(left, pad_r), (0, 0)])

    qb = pad_seq(q, 0).reshape(q.shape[:-2] + (nb, BAND_BLOCK, D))
    kp = pad_seq(k, BAND_BLOCK).reshape(k.shape[:-2] + (nb + 1, BAND_BLOCK, D))
    vp = pad_seq(v, BAND_BLOCK).reshape(v.shape[:-2] + (nb + 1, BAND_BLOCK, D))
    kb = jnp.concatenate([kp[..., :-1, :, :], kp[..., 1:, :, :]], axis=-2)
    vb = jnp.concatenate([vp[..., :-1, :, :], vp[..., 1:, :, :]], axis=-2)
    s = jnp.einsum('...gnid,...njd->...gnij', qb, kb,
                   preferred_element_type=jnp.float32) * (D ** -0.5)
    qi = jnp.arange(BAND_BLOCK)[:, None]
    kj = jnp.arange(2 * BAND_BLOCK)[None, :]
    dist = qi + BAND_BLOCK - kj
    kpos = jnp.arange(nb)[:, None, None] * BAND_BLOCK - BAND_BLOCK + kj
    mask = (dist >= 0) & (dist <= max_dist) & (kpos >= 0)
    s = jnp.where(mask, s, -jnp.inf)
    m = jnp.max(s, axis=-1)
    if sink is not None:
        sink = sink.astype(jnp.float32)[..., None, None]
        m = jnp.maximum(m, sink)
    p = jnp.exp(s - m[..., None])
    l = jnp.sum(p, axis=-1)
    if sink is not None:
        l = l + jnp.exp(sink - m)
    num = jnp.einsum('...gnij,...njd->...gnid', p.astype(v.dtype), vb,
                     preferred_element_type=jnp.float32)
    num = num.reshape(num.shape[:-3] + (nb * BAND_BLOCK, D))[..., :L, :]
    m = m.reshape(m.shape[:-2] + (nb * BAND_BLOCK,))[..., :L]
    l = l.reshape(l.shape[:-2] + (nb * BAND_BLOCK,))[..., :L]
    return num, m, l


def dilated_attention(q, k, v):
    B, H, S, Dh = q.shape
    nums, ms, ls = [], [], []
    for window, dil in A_BRANCHES:
        L = S // dil

        def to_res(t):
            return jnp.swapaxes(t.reshape(B, H, L, dil, Dh), 2, 3)

        num, m, l = banded_attention(to_res(q)[..., None, :, :], to_res(k), to_res(v),
                                     window // dil)
        nums.append(jnp.swapaxes(num[..., 0, :, :], 2, 3).reshape(B, H, S, Dh))
        ms.append(jnp.swapaxes(m[..., 0, :], 2, 3).reshape(B, H, S))
        ls.append(jnp.swapaxes(l[..., 0, :], 2, 3).reshape(B, H, S))
    m_all = jnp.stack(ms)
    w = jnp.exp(m_all - jnp.max(m_all, axis=0, keepdims=True))
    num = jnp.sum(w[..., None] * jnp.stack(nums), axis=0)
    den = jnp.sum(w * jnp.stack(ls), axis=0)
    return num / den[..., None]


def hgrn2_chunkwise(q, k, v, log_f):
    B, H, S, K = q.shape
    V = v.shape[-1]
    n = S // B_CHUNK

    def chunks(t):
        return jnp.moveaxis(t.reshape(B, H, n, B_CHUNK, t.shape[-1]), 2, 0)

    causal = jnp.tril(jnp.ones((B_CHUNK, B_CHUNK), bool))

    def step(state, xs):
        qc, kc, vc, gc = xs
        b = jnp.cumsum(gc, axis=-2)
        diff = b[..., :, None, :] - b[..., None, :, :]
        decay = jnp.exp(jnp.where(causal[..., None], diff, -jnp.inf))
        attn = jnp.einsum('bhtk,bhsk,bhtsk->bhts', qc, kc, decay)
        o = (jnp.einsum('bhts,bhsv->bhtv', attn, vc)
             + jnp.einsum('bhtk,bhkv->bhtv', qc * jnp.exp(b), state))
        b_last = b[..., -1:, :]
        new_state = (jnp.exp(b_last[..., 0, :])[..., None] * state
                     + jnp.einsum('bhsk,bhsv->bhkv', kc * jnp.exp(b_last - b), vc))
        return new_state, o

    state0 = jnp.zeros((B, H, K, V), jnp.float32)
    _, o = lax.scan(step, state0, (chunks(q), chunks(k), chunks(v), chunks(log_f)))
    return jnp.moveaxis(o, 0, 2).reshape(B, H, S, V)


def split_heads(t, n_heads):
    B, S, _ = t.shape
    return t.reshape(B, S, n_heads, -1).transpose(0, 2, 1, 3)


def merge_heads(t):
    B, H, S, Dh = t.shape
    return t.transpose(0, 2, 1, 3).reshape(B, S, H * Dh)


def even_mixer(h, w_in, w_out, lower_bound, out_norm_g, cos_a, sin_a):
    proj = h @ w_in
    qa, ka, va, qb, fb, ib, gb = jnp.split(proj, np.cumsum(EVEN_SPLITS)[:-1].tolist(), axis=-1)
    qa = apply_partial_rope(split_heads(qa, A_HEADS), cos_a, sin_a)
    ka = apply_partial_rope(split_heads(ka, A_HEADS), cos_a, sin_a)
    oa = dilated_attention(qa, ka, split_heads(va, A_HEADS))
    lb = lower_bound.astype(jnp.float32).reshape(B_HEADS, 1, B_KDIM)
    gate = lb + (1.0 - lb) * jax.nn.sigmoid(split_heads(fb, B_HEADS).astype(jnp.float32))
    q_b = jax.nn.silu(split_heads(qb, B_HEADS).astype(jnp.float32)) * (B_KDIM ** -0.5)
    ob = hgrn2_chunkwise(q_b, 1.0 - gate, split_heads(ib, B_HEADS).astype(jnp.float32),
                         jnp.log(gate))
    ob = rmsnorm(ob, out_norm_g) * jax.nn.silu(split_heads(gb, B_HEADS).astype(jnp.float32))
    mixed = jnp.concatenate([merge_heads(oa), merge_heads(ob)], axis=-1)
    return mixed.astype(h.dtype) @ w_out


def odd_mixer(h, w_qkv, b_qkv, sinks, w_o, b_o, cos_c, sin_c):
    B, S, _ = h.shape
    proj = h @ w_qkv + b_qkv
    q, k, v = jnp.split(proj, [C_Q_HEADS * C_DIM, (C_Q_HEADS + C_KV_HEADS) * C_DIM], axis=-1)
    q = apply_partial_rope(split_heads(q, C_Q_HEADS), cos_c, sin_c)
    q = q.reshape(B, C_KV_HEADS, C_GROUP, S, C_DIM)
    k = apply_partial_rope(split_heads(k, C_KV_HEADS), cos_c, sin_c)
    v = split_heads(v, C_KV_HEADS)
    num, _, l = banded_attention(q, k, v, C_WINDOW - 1,
                                 sink=sinks.reshape(C_KV_HEADS, C_GROUP))
    o = (num / l[..., None]).reshape(B, C_Q_HEADS, S, C_DIM)
    return merge_heads(o).astype(h.dtype) @ w_o + b_o


def squared_relu_mlp(h, w1, w2):
    return jnp.square(jax.nn.relu(h @ w1)) @ w2


def setup_inputs(seed: int = 0) -> dict:
    key = jax.random.key(seed)
    ks = jax.random.split(key, 15)

    def nrm(k, shape, scale):
        return scale * jax.random.normal(k, shape, jnp.float32)

    return {
        "x": nrm(ks[0], (BATCH, SEQ, D_MODEL), 1.0),
        "norm_mix_g": 1.0 + nrm(ks[1], (DEPTH, D_MODEL), 0.02),
        "norm_mlp_g": 1.0 + nrm(ks[2], (DEPTH, D_MODEL), 0.02),
        "final_norm_g": 1.0 + nrm(ks[3], (D_MODEL,), 0.02),
        "even_w_in": nrm(ks[4], (N_EVEN, D_MODEL, EVEN_IN), D_MODEL ** -0.5),
        "even_w_out": nrm(ks[5], (N_EVEN, EVEN_MIX, D_MODEL), EVEN_MIX ** -0.5),
        "hgrn_lb_raw": 1.0 + nrm(ks[6], (N_EVEN, B_KWIDTH), 0.1),
        "hgrn_norm_g": 1.0 + nrm(ks[7], (N_EVEN, B_VDIM), 0.02),
        "odd_w_qkv": nrm(ks[8], (N_ODD, D_MODEL, C_QKV), D_MODEL ** -0.5),
        "odd_b_qkv": nrm(ks[9], (N_ODD, C_QKV), 0.02),
        "odd_sinks": nrm(ks[10], (N_ODD, C_Q_HEADS), 1.0),
        "odd_w_o": nrm(ks[11], (N_ODD, C_Q_HEADS * C_DIM, D_MODEL), (C_Q_HEADS * C_DIM) ** -0.5),
        "odd_b_o": nrm(ks[12], (N_ODD, D_MODEL), 0.02),
        "mlp_w1": nrm(ks[13], (DEPTH, D_MODEL, D_FF), D_MODEL ** -0.5),
        "mlp_w2": nrm(ks[14], (DEPTH, D_FF, D_MODEL), D_FF ** -0.5),
    }


def reference(x, norm_mix_g, norm_mlp_g, final_norm_g, even_w_in, even_w_out, hgrn_lb_raw,
              hgrn_norm_g, odd_w_qkv, odd_b_qkv, odd_sinks, odd_w_o, odd_b_o, mlp_w1, mlp_w2):
    S = x.shape[1]
    cos_a, sin_a = rope_tables(S, A_DIM)
    cos_c, sin_c = rope_tables(S, C_DIM)
    lb_soft = jax.nn.softmax(hgrn_lb_raw.astype(jnp.float32), axis=0)
    lower_bounds = jnp.cumsum(lb_soft, axis=0) - lb_soft[0:1]
    for layer in range(DEPTH):
        h = rmsnorm(x, norm_mix_g[layer])
        if layer % 2 == 0:
            e = layer // 2
            mix = even_mixer(h, even_w_in[e], even_w_out[e], lower_bounds[e], hgrn_norm_g[e],
                             cos_a, sin_a)
        else:
            o = layer // 2
            mix = odd_mixer(h, odd_w_qkv[o], odd_b_qkv[o], odd_sinks[o], odd_w_o[o], odd_b_o[o],
                            cos_c, sin_c)
        x = x + mix.astype(x.dtype)
        h = rmsnorm(x, norm_mlp_g[layer])
        x = x + squared_relu_mlp(h, mlp_w1[layer], mlp_w2[layer]).astype(x.dtype)
    return rmsnorm(x, final_norm_g)
```

```python
import contextlib
import numpy as np
import concourse.bass as bass
import concourse.mybir as mybir
from concourse.bass_utils import run_bass_kernel_spmd

F32 = mybir.dt.float32
BF16 = mybir.dt.bfloat16
AF = mybir.ActivationFunctionType
ALU = mybir.AluOpType

D = 2048
SEQ = 2048
NKC = 16
TB = 512
NTB = 4
EPS = 1e-5
NEG = -30000.0
SAME_ENGINE_SYNC = True


class _Res:
    __slots__ = ("w", "r", "rd")

    def __init__(self):
        self.w = None
        self.r = {}
        self.rd = []


class _Op:
    __slots__ = ("eng", "fn", "deps", "sig", "sem", "val", "dma", "idx")


class Sched:
    ENG = ("pe", "act", "dve", "pool", "sp")

    def __init__(self, nc):
        self.nc = nc
        self.ops = {e: [] for e in self.ENG}
        self.res = {}
        self.dkeys = {}
        self.pending = {e: [] for e in self.ENG}
        self.last = {e: None for e in self.ENG}
        self.n = 0

    def R(self, key):
        r = self.res.get(key)
        if r is None:
            r = self.res[key] = _Res()
        return r

    def _mk(self, eng, fn, reads, writes, dma):
        op = _Op()
        op.eng, op.fn, op.sig, op.sem, op.val, op.dma = eng, fn, False, None, 0, dma
        op.idx = self.n
        self.n += 1
        deps = {}
        for k in reads:
            r = self.R(k)
            if r.w is not None:
                deps[id(r.w)] = r.w
        for k in writes:
            r = self.R(k)
            if r.w is not None:
                deps[id(r.w)] = r.w
            for o in r.r.values():
                deps[id(o)] = o
            for o in r.rd:
                deps[id(o)] = o
        for k in reads:
            r = self.R(k)
            if dma:
                r.rd.append(op)
            else:
                r.r[eng] = op
        for k in writes:
            r = self.R(k)
            r.w = op
            r.r = {}
            r.rd = []
        dl = []
        for d in deps.values():
            if d is op:
                continue
            if d.dma or dma or d.eng != eng:
                dl.append(d)
            elif SAME_ENGINE_SYNC and eng != "pe":
                dl.append(d)
        if self.pending[eng] and not (dma and eng == "pool"):
            have = set(id(d) for d in dl)
            for d in self.pending[eng]:
                if id(d) not in have and d is not op:
                    dl.append(d)
            self.pending[eng] = []
        op.deps = dl
        for d in dl:
            d.sig = True
        self.ops[eng].append(op)
        if not dma:
            self.last[eng] = op
        return op

    def add(self, eng, fn, reads=(), writes=()):
        return self._mk(eng, fn, reads, writes, False)

    def dma(self, eng, fn, ndma, key, reads=(), writes=()):
        op = self._mk(eng, fn, reads, writes, True)
        ent = self.dkeys.get(key)
        if ent is None:
            ent = self.dkeys[key] = [None, 0, None]
        if ent[2] is not None and all(d is not ent[2] for d in op.deps):
            op.deps.append(ent[2])
        ent[1] += 16 * ndma
        ent[2] = op
        op.sem = key
        op.val = ent[1]
        op.sig = True
        return op

    def barrier(self, engines=("pe", "act", "dve", "sp")):
        lasts = [self.last[e] for e in self.ENG if self.last[e] is not None]
        dl = [ent[2] for ent in self.dkeys.values() if ent[2] is not None]
        for e in engines:
            self.pending[e] = [d for d in lasts if d.eng != e] + dl

    def emit(self, final_waits=()):
        nc = self.nc
        with contextlib.ExitStack() as es:
            esems = {e: es.enter_context(nc.semaphore("s_" + e)) for e in self.ENG}
            dsems = {k: es.enter_context(nc.semaphore("d_%d" % i)) for i, k in enumerate(self.dkeys)}
            for e in self.ENG:
                c = 0
                for op in self.ops[e]:
                    if op.dma:
                        op.sem = dsems[op.sem]
                    elif op.sig:
                        c += 1
                        op.sem = esems[e]
                        op.val = c
            block = es.enter_context(nc.Block())

            def run(engh, e, extra=()):
                waited = {}
                for op in self.ops[e]:
                    for d in op.deps:
                        if waited.get(id(d.sem), 0) < d.val:
                            engh.wait_ge(d.sem, d.val)
                            waited[id(d.sem)] = d.val
                    if op.dma:
                        op.fn(engh, op.sem)
                    else:
                        ins = op.fn(engh)
                        if op.sig:
                            ins.then_inc(op.sem, 1)
                for d in extra:
                    if waited.get(id(d.sem), 0) < d.val:
                        engh.wait_ge(d.sem, d.val)
                        waited[id(d.sem)] = d.val

            @block.tensor
            def _(t):
                run(t, "pe")

            @block.scalar
            def _(t):
                run(t, "act")

            @block.vector
            def _(t):
                run(t, "dve")

            @block.gpsimd
            def _(t):
                run(t, "pool")

            @block.sync
            def _(t):
                run(t, "sp", final_waits)


def _host_consts():
    c = {}
    ki = np.arange(128)[:, None]
    qi = np.arange(128)[None, :]
    mA = np.zeros((128, 256), np.float32)
    mA[:, :128] = np.where(qi >= ki, 0.0, NEG)
    mA[:, 128:] = np.where(qi <= ki, 0.0, NEG)
    mC = np.zeros((128, 256), np.float32)
    mC[:, :128] = np.where(qi >= ki, 0.0, NEG)
    mC[:, 128:] = np.where(qi < ki, 0.0, NEG)
    ident = np.eye(128, dtype=np.float32)
    s = np.arange(128)[:, None]
    t = np.arange(128)[None, :]
    same = (s // 64) == (t // 64)
    sl, tl = s % 64, t % 64
    mid = 32
    trimid = np.where(same, (sl <= tl).astype(np.float32) - (sl <= mid).astype(np.float32), 0.0)
    tristart = np.where(same & (sl <= tl), 1.0, 0.0)
    after = np.where(same & (sl > tl), 1.0, 0.0)
    caus = np.where(sl <= tl, 1.0, 0.0)[:, :64]
    caus = np.tile(caus, (1, 4))
    permA = np.zeros((128, 128), np.float32)
    for m in range(16):
        permA[m + 16, m] = 1.0
        permA[m, m + 16] = 1.0
    permC = np.zeros((128, 128), np.float32)
    for base in (0, 64):
        for m in range(8):
            permC[base + m + 8, base + m] = 1.0
            permC[base + m, base + m + 8] = 1.0
    c["cb16"] = np.ascontiguousarray(np.concatenate([mA, mC, ident, caus, permA, permC], axis=1), np.float32)
    c["cf32"] = np.ascontiguousarray(np.concatenate([trimid, tristart, after, ident], axis=1), np.float32)

    def tables(head_dim):
        rot = head_dim // 4
        inv = (1.0 / (np.float32(500000.0) ** (np.arange(0, rot, 2, dtype=np.float32) / np.float32(rot)))).astype(np.float32)
        ang = (np.arange(SEQ, dtype=np.float32)[:, None] * inv[None, :]).astype(np.float32)
        return np.cos(ang).astype(np.float32).T, np.sin(ang).astype(np.float32).T

    ca, sa = tables(128)
    ropeA = np.zeros((2, 128, SEQ), np.float32)
    ropeA[0, :, :] = 1.0
    ropeA[0, 0:16] = ca
    ropeA[0, 16:32] = ca
    ropeA[1, 0:16] = -sa
    ropeA[1, 16:32] = sa
    cc, sc = tables(64)
    ropeC = np.zeros((2, 128, SEQ), np.float32)
    ropeC[0, :, :] = 1.0
    for base in (0, 64):
        ropeC[0, base:base + 8] = cc
        ropeC[0, base + 8:base + 16] = cc
        ropeC[1, base:base + 8] = -sc
        ropeC[1, base + 8:base + 16] = sc
    c["ropeA"] = ropeA
    c["ropeC"] = ropeC
    return c


def build(cfg):
    layers = cfg["layers"]
    final = cfg.get("final", True)
    nc = bass.Bass("TRN2", target_bir_lowering=False)

    def din(name, shape, dt=F32):
        return nc.dram_tensor(name, list(shape), dt, kind="ExternalInput").ap()

    xT = din("xT", [NTB, 128, NKC * TB])
    gains = din("gains", [128, 9 * 16])
    cb16_d = din("cb16", [128, 1152])
    cf32_d = din("cf32", [128, 512])
    ropeA_d = din("ropeA", [2, 128, SEQ])
    ropeC_d = din("ropeC", [2, 128, SEQ])
    used_e = sorted(set(sp["l"] // 2 for sp in layers if sp["mixer"] == "even"))
    used_o = sorted(set(sp["l"] // 2 for sp in layers if sp["mixer"] == "odd"))
    used_m = sorted(set(sp["l"] for sp in layers if sp["mlp"]))
    w_evA = {e: din("w_evA%d" % e, [8, 128, 16, 384]) for e in used_e}
    w_evB = {e: din("w_evB%d" % e, [8, 128, 16, 512]) for e in used_e}
    w_evO = {e: din("w_evO%d" % e, [4, 128, 16, 512]) for e in used_e}
    w_odQ = {o: din("w_odQ%d" % o, [4, 128, 16, 512]) for o in used_o}
    w_odKV = {o: din("w_odKV%d" % o, [4, 128, 16, 192]) for o in used_o}
    w_odO = {o: din("w_odO%d" % o, [4, 128, 16, 512]) for o in used_o}
    w1p = {l: din("w1p%d" % l, [16, 128, 16, 512]) for l in used_m}
    w2p = {l: din("w2p%d" % l, [2, 8, 128, 32, 256]) for l in used_m}
    smalls = din("smalls", [128, 256])
    outT = nc.dram_tensor("outT", [NTB, 128, NKC * TB], F32, kind="ExternalOutput").ap()
    XS = nc.dram_tensor("xs_scr", [NTB, 128, NKC * TB], F32).ap()
    MIXD = nc.dram_tensor("mix_scr", [NTB, 128, NKC, TB], BF16).ap()
    MIXW = MIXD.rearrange("tb p c t -> p tb c t")

    SM_LBRAW = 0
    SM_GNORM = 16
    SM_BQ = 18
    SM_BK = 50
    SM_BV = 58
    SM_SINK = 66
    SM_BO = 98

    KB = 1024
    with contextlib.ExitStack() as es:
        arena = es.enter_context(nc.sbuf_tensor("arena", [128, 194 * KB // 4], F32))
        csb = es.enter_context(nc.sbuf_tensor("csb", [128, 1152], BF16))
        cfs = es.enter_context(nc.sbuf_tensor("cfs", [128, 512], F32))
        gsb = es.enter_context(nc.sbuf_tensor("gsb", [128, 9 * 16], F32))
        sms = es.enter_context(nc.sbuf_tensor("sms", [128, 256], F32))
        misc = es.enter_context(nc.sbuf_tensor("misc", [128, 512], F32))
        onesb = es.enter_context(nc.sbuf_tensor("onesb", [128, 128], BF16))
        ps = [es.enter_context(nc.psum_tensor("ps%d" % i, [128, 512], F32)) for i in range(8)]
        S = Sched(nc)

        def view(off, nbytes, dt):
            assert off % 4 == 0 and nbytes % 4 == 0 and off + nbytes <= 194 * KB, (off, nbytes)
            v = arena[:, off // 4:(off + nbytes) // 4]
            return v if dt == F32 else v.bitcast(dt)

        W_OFF, HB_OFF, T_OFF = 0, 64 * KB, 128 * KB
        T_SIZE = 66 * KB
        wslots = [view(W_OFF + i * 16 * KB, 16 * KB, BF16) for i in range(4)]
        HB = view(HB_OFF, 64 * KB, BF16).rearrange("p (kc t) -> p kc t", kc=NKC)

        maskA = csb[:, 0:256]
        maskC = csb[:, 256:512]
        identb = csb[:, 512:640]
        caus = csb[:, 640:896]
        permA = csb[:, 896:1024]
        permC = csb[:, 1024:1152]
        trimid = cfs[:, 0:128]
        tristart = cfs[:, 128:256]
        after = cfs[:, 256:384]
        identf = cfs[:, 384:512]
        onesf = misc[:, 0:128]
        epsc = misc[:, 128:129]
        lbc = misc[:, 136:144]
        omlc = misc[:, 144:152]
        esink = misc[:, 160:192]
        zcol = misc[:, 200:201]

        def MM(out, lhsT, rhs, start, stop, R, W):
            S.add("pe", lambda e: e.matmul(out, lhsT=lhsT, rhs=rhs, start=start, stop=stop), R, W)

        def ACT(out, in_, func, R, W, **kw):
            S.add("act", lambda e: e.activation(out=out, in_=in_, func=func, **kw), R, W)

        def TT(eng, out, in0, in1, op, R, W):
            S.add(eng, lambda e: e.tensor_tensor(out=out, in0=in0, in1=in1, op=op), R, W)

        def STT(eng, out, in0, scalar, in1, op0, op1, R, W):
            S.add(eng, lambda e: e.scalar_tensor_tensor(out=out, in0=in0, scalar=scalar, in1=in1, op0=op0, op1=op1), R, W)

        def TS(eng, out, in0, s1, s2, op0, op1, R, W):
            S.add(eng, lambda e: e.tensor_scalar(out=out, in0=in0, scalar1=s1, scalar2=s2, op0=op0, op1=op1), R, W)

        def CP(eng, out, in_, R, W):
            S.add(eng, lambda e: e.tensor_copy(out=out, in_=in_), R, W)

        def RECIP(out, in_, R, W):
            S.add("dve", lambda e: e.reciprocal(out=out, in_=in_), R, W)

        def MEMSET(eng, ap, val, W):
            S.add(eng, lambda e: e.memset(ap, val), (), W)

        def DMA(eng, out, in_, key, R, W):
            return S.dma(eng, lambda e, s: e.dma_start(out=out, in_=in_).then_inc(s, 16), 1, key, R, W)

        st = {"bank": 0, "w": 0, "alt": 0}

        def nextbank():
            i = st["bank"] % 3
            st["bank"] += 1
            return ps[i][:], ("ps", i)

        def alt():
            st["alt"] += 1
            return "dve"

        def wload(src, shape):
            i = st["w"] % 4
            st["w"] += 1
            n = int(np.prod(shape))
            v = wslots[i][:, 0:n]
            DMA("pool", v, src.rearrange("p a b -> p (a b)"), ("W", i), (), [("W", i)])
            v = v.rearrange("p (a b) -> p a b", a=shape[0])
            return v, ("W", i)

        DMA("pool", csb[:], cb16_d, "c0", (), ["csb"])
        DMA("sp", cfs[:], cf32_d, "c1", (), ["cfs"])
        DMA("sp", gsb[:], gains, "c2", (), ["gsb"])
        DMA("sp", sms[:], smalls, "c3", (), ["sms"])
        MEMSET("dve", misc[:], 0.0, ["misc"])
        MEMSET("dve", onesf, 1.0, ["misc"])
        MEMSET("dve", epsc, EPS, ["misc"])
        MEMSET("dve", onesb[:], 1.0, ["onesb"])
        TT("dve", lbc, sms[:, SM_LBRAW + 8:SM_LBRAW + 16], sms[:, SM_LBRAW:SM_LBRAW + 8], ALU.subtract, ["sms"], ["misc"])
        ACT(lbc, lbc, AF.Sigmoid, ["misc"], ["misc"])
        TS("dve", omlc, lbc, -1.0, 1.0, ALU.mult, ALU.add, ["misc"], ["misc"])
        ACT(esink, sms[:, SM_SINK:SM_SINK + 32], AF.Exp, ["sms"], ["misc"])
        CONST = ["csb", "cfs", "gsb", "sms", "misc", "onesb"]

        def norm_block(xv, xkey, gcol0, outv, outkey, tv, inplace_f32=False):
            bank, bkey = nextbank()
            for kc in range(NKC):
                sq, sqk = tv["sq"][kc % 2], ("sq", kc % 2)
                ACT(sq, xv[:, kc, :], AF.Square, [xkey], [sqk])
                MM(bank, onesf, sq, kc == 0, kc == NKC - 1, [sqk, "misc"], [bkey])
            ACT(tv["rt"], bank, AF.Sqrt, [bkey, "misc"], ["rt"], bias=epsc, scale=1.0 / D)
            RECIP(tv["rr"], tv["rt"], ["rt"], ["rr"])
            for kc in range(NKC):
                STT("dve", outv[:, kc, :], xv[:, kc, :], gsb[:, gcol0 + kc:gcol0 + kc + 1], tv["rr"],
                    ALU.mult, ALU.mult, [xkey, "rr", "gsb"], [outkey])

        def phase_norm(li, l):
            xb = view(T_OFF, 32 * KB, F32).rearrange("p (kc t) -> p kc t", kc=NKC)
            tv = {"sq": [view(T_OFF + 32 * KB + i * 2 * KB, 2 * KB, F32) for i in range(2)],
                  "rt": view(T_OFF + 36 * KB, 2 * KB, F32), "rr": view(T_OFF + 38 * KB, 2 * KB, F32)}
            src = xT if li == 0 else XS
            for tb in range(NTB):
                DMA("sp", xb.rearrange("p kc t -> p (kc t)"), src[tb], "xn", [("XS", tb)], ["xn"])
                norm_block(xb, "xn", l * 16, HB[:, :, tb * TB:(tb + 1) * TB], ("HB", tb), tv)

        HBK = [("HB", tb) for tb in range(NTB)]

        def project(wv, wk, col0, evac):
            for tb in range(NTB):
                bank, bkey = nextbank()
                for kc in range(NKC):
                    MM(bank, wv[:, kc, col0:col0 + 128], HB[:, kc, tb * TB:(tb + 1) * TB], kc == 0, kc == NKC - 1,
                       [wk, ("HB", tb)], [bkey])
                evac(tb, bank, bkey)

        def rope(dst, dkey, tb, perm_lhsT, nrow, ropeC_t, ropeS_t, t1, t2):
            blk = slice(tb * TB, (tb + 1) * TB)
            bank, bkey = nextbank()
            MM(bank[0:nrow, :], perm_lhsT[:, 0:nrow], dst[:, blk], True, True, [dkey, "csb"], [bkey])
            TT("dve", t1[0:nrow, :], bank[0:nrow, :], ropeS_t[0:nrow, blk], ALU.mult, [bkey, "rope"], ["rt1"])
            TT("dve", t2[0:nrow, :], dst[0:nrow, blk], ropeC_t[0:nrow, blk], ALU.mult, [dkey, "rope"], ["rt2"])
            TT("dve", dst[0:nrow, blk], t1[0:nrow, :], t2[0:nrow, :], ALU.add, ["rt1", "rt2"], [dkey])

        pss = [(ps[3 + i][:, 0:256], ("pss", i)) for i in range(3)]

        def attn_block(cnt, kT_cur, kT_prev, q_ap, v_cur, v_prev, mask, scale, PT, po, pl, prow, R_in, ones_l, si):
            sv, sk = pss[cnt % 3 if si is None else si]
            pt, ptk = PT[cnt % 4], ("PT", cnt % 4)
            hp = kT_prev is not None
            ncol = 256 if hp else 128

            def scores():
                MM(sv[:, 0:ncol], identb, mask[:, 0:ncol], True, False, ["csb"], [sk])
                MM(sv[:, 0:128], kT_cur, q_ap, False, not hp, R_in, [sk])
                if hp:
                    MM(sv[:, 128:256], kT_prev, q_ap, False, True, R_in, [sk])
                ACT(pt[:, 0:ncol], sv[:, 0:ncol], AF.Exp, [sk], [ptk], scale=scale)

            def pv():
                MM(po, v_cur, pt[:, 0:128], True, not hp, [ptk] + R_in, [prow[0]])
                if hp:
                    MM(po, v_prev, pt[:, 128:256], False, True, [ptk] + R_in, [prow[0]])
                MM(pl, ones_l, pt[:, 0:128], True, not hp, [ptk, "onesb"], [prow[1]])
                if hp:
                    MM(pl, ones_l, pt[:, 128:256], False, True, [ptk, "onesb"], [prow[1]])
            return scores, pv

        def attn_pipeline(blocks, skew=2):
            n = len(blocks)
            for i in range(n + skew):
                if i < n:
                    blocks[i][0]()
                k = i - skew
                if k >= 0:
                    blocks[k][1]()
                    if blocks[k][2] is not None:
                        blocks[k][2]()

        def even_A(e, h):
            o = T_OFF
            qT = view(o, 4 * KB, BF16); o += 4 * KB
            kT = view(o, 4 * KB, BF16); o += 4 * KB
            vT = view(o, 4 * KB, BF16); o += 4 * KB
            vtok = []
            for b in range(3):
                vtok.append(view(o, 4 * KB, BF16).rearrange("p (j d) -> p j d", j=16)); o += 4 * KB
            num = view(o, 8 * KB, F32); o += 8 * KB
            lac = view(o, 8 * KB, F32); o += 8 * KB
            PT = [view(o + i * 512, 512, BF16) for i in range(4)]; o += 2 * KB
            rC = view(o, 8 * KB, F32); o += 8 * KB
            rS = view(o, 8 * KB, F32); o += 8 * KB
            t1 = view(o, 2 * KB, F32); o += 2 * KB
            t2 = view(o, 2 * KB, F32); o += 2 * KB
            oa = view(o, 4 * KB, BF16); o += 4 * KB
            assert o <= T_OFF + T_SIZE
            DMA("sp", rC[0:32, :], ropeA_d[0, 0:32, :], "rope", (), ["rope"])
            DMA("sp", rS[0:32, :], ropeA_d[1, 0:32, :], "rope", (), ["rope"])
            wv, wk = wload(w_evA[e][h], (16, 384))
            for ci, (dst, dk) in enumerate(((qT, "qT"), (kT, "kT"))):
                def ev(tb, bank, bkey, dst=dst, dk=dk):
                    ACT(dst[:, tb * TB:(tb + 1) * TB], bank, AF.Copy, [bkey], [dk])
                    rope(dst, dk, tb, permA, 32, rC, rS, t1, t2)
                project(wv, wk, ci * 128, ev)

            def evv(tb, bank, bkey):
                ACT(vT[:, tb * TB:(tb + 1) * TB], bank, AF.Copy, [bkey], ["vT"])
            project(wv, wk, 256, evv)
            branches = (1, 4, 16)

            def tok_slice(dil, r, jb):
                s0 = dil * 128 * jb + r
                return slice(s0, s0 + dil * 127 + 1, dil)

            for b, dil in enumerate(branches):
                nbc = 16 // dil
                for g4 in range(4):
                    bank, bkey = nextbank()
                    for j in range(4):
                        bi = g4 * 4 + j
                        r, jb = bi // nbc, bi % nbc
                        MM(bank[:, j * 128:(j + 1) * 128], vT[:, tok_slice(dil, r, jb)], identb, True, True,
                           ["vT", "csb"], [bkey])
                    CP("dve", vtok[b][:, g4 * 4:g4 * 4 + 4, :], bank.rearrange("p (j d) -> p j d", j=4), [bkey], [("vtok", b)])
            scale = 128.0 ** -0.5
            cnt = 0
            blocks = []
            for b, dil in enumerate(branches):
                nbc = 16 // dil
                for g4 in range(4):
                    grp = b * 4 + g4
                    pob, pok = ps[6 + grp % 2][:], ("ps", 6 + grp % 2)
                    plb, plk = nextbank()
                    for j in range(4):
                        bi = g4 * 4 + j
                        r, jb = bi // nbc, bi % nbc
                        hp = jb > 0
                        sc_, pv_ = attn_block(cnt, kT[:, tok_slice(dil, r, jb)],
                                              kT[:, tok_slice(dil, r, jb - 1)] if hp else None,
                                              qT[:, tok_slice(dil, r, jb)],
                                              vtok[b][:, bi, :], vtok[b][:, bi - 1, :] if hp else None,
                                              maskA, scale, PT, pob[:, j * 128:(j + 1) * 128], plb[:, j * 128:(j + 1) * 128],
                                              (pok, plk), ["qT", "kT", ("vtok", b)], onesb[:], None)
                        cnt += 1
                        post = None
                        if j == 3:
                            def post(b=b, dil=dil, g4=g4, pob=pob, plb=plb, pok=pok, plk=plk):
                                if dil == 1:
                                    dn, dl_ = num[:, g4 * 512:(g4 + 1) * 512], lac[:, g4 * 512:(g4 + 1) * 512]
                                    sn, sl_ = pob, plb
                                elif dil == 4:
                                    dn, dl_ = num[:, g4::4], lac[:, g4::4]
                                    sn, sl_ = pob, plb
                                else:
                                    dn = num.rearrange("d (p r) -> d r p", r=16)[:, g4 * 4:g4 * 4 + 4, :]
                                    dl_ = lac.rearrange("d (p r) -> d r p", r=16)[:, g4 * 4:g4 * 4 + 4, :]
                                    sn = pob.rearrange("d (a p) -> d a p", a=4)
                                    sl_ = plb.rearrange("d (a p) -> d a p", a=4)
                                if b == 0:
                                    CP("dve", dn, sn, [pok], ["num"])
                                    ACT(dl_, sl_, AF.Copy, [plk], ["lac"])
                                else:
                                    TT("dve", dn, dn, sn, ALU.add, [pok, "num"], ["num"])
                                    TT("dve", dl_, dl_, sl_, ALU.add, [plk, "lac"], ["lac"])
                        blocks.append((sc_, pv_, post))
            attn_pipeline(blocks)
            RECIP(lac, lac, ["lac"], ["lac"])
            TT("dve", oa, num, lac, ALU.mult, ["num", "lac"], ["oa"])
            DMA("sp", MIXW[:, :, h, :], oa.rearrange("p (tb t) -> p tb t", tb=NTB), "mixst", ["oa"], [("MIXD", h)])

        def even_B(e, h):
            o = T_OFF
            def al(n, dt):
                nonlocal o
                v = view(o, n, dt)
                o += n
                return v
            qs = al(8 * KB, F32)
            gsil = al(4 * KB, BF16)
            g_reg = o
            gT = al(8 * KB, F32)
            Sb = view(g_reg, 8 * KB, BF16).rearrange("p (c v) -> p c v", c=32)
            kk_reg = o
            kkT = al(4 * KB, BF16)
            ob = view(kk_reg, 4 * KB, BF16)
            i_reg = o
            iT = al(4 * KB, BF16)
            khat = view(i_reg, 4 * KB, BF16).rearrange("p (j k) -> p j k", j=16)
            gtok = al(8 * KB, F32).rearrange("p (j k) -> p j k", j=16)
            kktok = al(4 * KB, BF16).rearrange("p (j k) -> p j k", j=16)
            itok = al(4 * KB, BF16).rearrange("p (j k) -> p j k", j=16)
            qtil = al(4 * KB, BF16)
            ktil = al(4 * KB, BF16)
            qd = al(4 * KB, BF16)
            tmp = [al(2 * KB, F32) for _ in range(4)]
            Sst = al(512, F32)
            dec = al(128, F32)
            am = [al(512, BF16) for _ in range(2)]
            assert o <= T_OFF + T_SIZE, o - T_OFF
            KG, KK, KI = "B_g", "B_kk", "B_i"
            wv, wk = wload(w_evB[e][h], (16, 512))
            def ev_q(tb, bank, bkey):
                ACT(qs[:, tb * TB:(tb + 1) * TB], bank, AF.Silu, [bkey], ["qs"])
            project(wv, wk, 0, ev_q)
            def ev_g(tb, bank, bkey):
                ACT(gsil[:, tb * TB:(tb + 1) * TB], bank, AF.Silu, [bkey], ["gsil"])
            project(wv, wk, 128, ev_g)
            def ev_f(tb, bank, bkey):
                ACT(gT[:, tb * TB:(tb + 1) * TB], bank, AF.Sigmoid, [bkey], [KG])
            project(wv, wk, 256, ev_f)
            def ev_i(tb, bank, bkey):
                ACT(iT[:, tb * TB:(tb + 1) * TB], bank, AF.Copy, [bkey], [KI])
            project(wv, wk, 384, ev_i)
            bstop = cfg.get("bstop", 99)
            if bstop <= 0:
                return
            if e == 1:
                TS("dve", gT, gT, omlc[:, h:h + 1], lbc[:, h:h + 1], ALU.mult, ALU.add, [KG, "misc"], [KG])
            TS("dve", kkT, gT, -1.0, 1.0, ALU.mult, ALU.add, [KG], [KK])
            ACT(gT, gT, AF.Ln, [KG], [KG])
            if bstop <= 1:
                return
            for g4 in range(4):
                b1, k1 = nextbank()
                b2, k2 = nextbank()
                b3, k3 = nextbank()
                for j in range(4):
                    tt = g4 * 4 + j
                    ts_ = slice(tt * 128, (tt + 1) * 128)
                    MM(b1[:, j * 128:(j + 1) * 128], gT[:, ts_], identf, True, True, [KG, "cfs"], [k1])
                    MM(b2[:, j * 128:(j + 1) * 128], kkT[:, ts_], identb, True, True, [KK, "csb"], [k2])
                    MM(b3[:, j * 128:(j + 1) * 128], iT[:, ts_], identb, True, True, [KI, "csb"], [k3])
                r4 = lambda bk: bk.rearrange("p (j d) -> p j d", j=4)
                CP("dve", gtok[:, g4 * 4:g4 * 4 + 4, :], r4(b1), [k1], ["gtok"])
                ACT(kktok[:, g4 * 4:g4 * 4 + 4, :], r4(b2), AF.Copy, [k2], ["kktok"])
                CP("dve", itok[:, g4 * 4:g4 * 4 + 4, :], r4(b3), [k3], ["itok"])
            if bstop <= 2:
                return
            for g4 in range(4):
                blk = slice(g4 * 512, (g4 + 1) * 512)
                b1, k1 = nextbank()
                for j in range(4):
                    MM(b1[:, j * 128:(j + 1) * 128], gtok[:, g4 * 4 + j, :], trimid, True, True, ["gtok", "cfs"], [k1])
                ACT(tmp[0], b1, AF.Exp, [k1], ["tmp0"])
                ACT(tmp[1], b1, AF.Exp, [k1], ["tmp1"], scale=-1.0)
                TT("dve", qtil[:, blk], qs[:, blk], tmp[0], ALU.mult, ["qs", "tmp0"], ["qtil"])
                TT("dve", ktil[:, blk], kkT[:, blk], tmp[1], ALU.mult, [KK, "tmp1"], ["ktil"])
                b2, k2 = nextbank()
                for j in range(4):
                    MM(b2[:, j * 128:(j + 1) * 128], gtok[:, g4 * 4 + j, :], tristart, True, True, ["gtok", "cfs"], [k2])
                ACT(tmp[2], b2, AF.Exp, [k2], ["tmp2"])
                TT("dve", qd[:, blk], qs[:, blk], tmp[2], ALU.mult, ["qs", "tmp2"], ["qd"])
                CP("dve", dec[:, g4 * 8:(g4 + 1) * 8], tmp[2][:, 63::64], ["tmp2"], ["dec"])
            for g4 in range(4):
                b3, k3 = nextbank()
                for j in range(4):
                    MM(b3[:, j * 128:(j + 1) * 128], after, gtok[:, g4 * 4 + j, :], True, True, ["gtok", "cfs"], [k3])
                ACT(tmp[3], b3, AF.Exp, [k3], ["tmp3"])
                TT("dve", khat[:, g4 * 4:g4 * 4 + 4, :], kktok[:, g4 * 4:g4 * 4 + 4, :],
                   tmp[3].rearrange("p (j d) -> p j d", j=4), ALU.mult, ["kktok", "tmp3"], [KI])
            if bstop <= 3:
                return
            MEMSET("dve", Sst, 0.0, ["Sst"])
            MEMSET("dve", Sb[:, 0, :], 0.0, [KG])
            for g4 in range(4):
                bks = [nextbank(), nextbank()]
                for jj in range(4):
                    for par in range(2):
                        c = g4 * 8 + jj * 2 + par
                        tt, p0 = c // 2, 64 * par
                        MM(bks[par][0][:, jj * 128:(jj + 1) * 128], khat[p0:p0 + 64, tt, :], itok[p0:p0 + 64, tt, :],
                           True, True, [KI, "itok"], [bks[par][1]])
                for par in range(2):
                    ACT(tmp[par], bks[par][0], AF.Copy, [bks[par][1]], ["tmp%d" % par])
                for jj in range(4 if cfg.get("b4", 0) != 1 else 0):
                    for par in range(2):
                        c = g4 * 8 + jj * 2 + par
                        if c == 31:
                            break
                        STT("dve", Sst, Sst, dec[:, c:c + 1], tmp[par][:, jj * 128:(jj + 1) * 128], ALU.mult, ALU.add,
                            ["Sst", "dec", "tmp%d" % par], ["Sst"])
                        ACT(Sb[:, c + 1, :], Sst, AF.Copy, ["Sst"], [KG])
            if bstop <= 4:
                return
            cK = 128.0 ** -0.5
            for g4 in range(4):
                blk = slice(g4 * 512, (g4 + 1) * 512)
                for half in range(2):
                    sv, sk = pss[(g4 * 2 + half) % 3]
                    amv, amk = am[half], ("am", half)
                    for j in range(4):
                        c = g4 * 8 + half * 4 + j
                        p0 = 64 * (c % 2)
                        cs = slice(c * 64, (c + 1) * 64)
                        MM(sv[p0:p0 + 64, (j // 2) * 64:(j // 2) * 64 + 64], ktil[:, cs], qtil[:, cs], True, True,
                           ["ktil", "qtil"], [sk])
                    TT("dve", amv[:, 0:128], sv[:, 0:128], caus[:, 0:128], ALU.mult, [sk, "csb"], [amk])
                    for j in range(4):
                        c = g4 * 8 + half * 4 + j
                        tt, p0 = c // 2, 64 * (c % 2)
                        cs = slice(c * 64, (c + 1) * 64)
                        par = c % 2
                        oc = ((half * 4 + j) // 2) * 64
                        pb, pk = ps[6 + par], ("ps", 6 + par)
                        MM(pb[:, oc:oc + 64], itok[p0:p0 + 64, tt, :], amv[p0:p0 + 64, (j // 2) * 64:(j // 2) * 64 + 64],
                           True, False, ["itok", amk], [pk])
                        MM(pb[:, oc:oc + 64], Sb[:, c, :], qd[:, cs], False, True, [KG, "qd"], [pk])
                for par in range(2):
                    srcv = ps[6 + par][:, 0:256].rearrange("p (i t) -> p i t", t=64)
                    d0 = tmp[0].rearrange("p (i two t) -> p i two t", two=2, t=64)[:, :, par, :]
                    d1 = tmp[1].rearrange("p (i two t) -> p i two t", two=2, t=64)[:, :, par, :]
                    ACT(d0, srcv, AF.Square, [("ps", 6 + par)], ["tmp0"], scale=cK)
                    ACT(d1, srcv, AF.Copy, [("ps", 6 + par)], ["tmp1"], scale=cK)
                bank, bkey = nextbank()
                MM(bank, onesf, tmp[0], True, True, ["tmp0", "misc"], [bkey])
                ACT(tmp[2], bank, AF.Sqrt, [bkey, "misc"], ["tmp2"], bias=epsc, scale=1.0 / 128.0)
                RECIP(tmp[2], tmp[2], ["tmp2"], ["tmp2"])
                STT("dve", tmp[3], tmp[1], sms[:, SM_GNORM + e:SM_GNORM + e + 1], tmp[2], ALU.mult, ALU.mult,
                    ["tmp1", "tmp2", "sms"], ["tmp3"])
                TT("dve", ob[:, blk], tmp[3], gsil[:, blk], ALU.mult, ["tmp3", "gsil"], [KK])
            DMA("sp", MIXW[:, :, 8 + h, :], ob.rearrange("p (tb t) -> p tb t", tb=NTB), "mixst", [KK], [("MIXD", 8 + h)])

        def odd_group(oi, g):
            o = T_OFF
            def al(n, dt):
                nonlocal o
                v = view(o, n, dt)
                o += n
                return v
            qc = [al(4 * KB, BF16) for _ in range(4)]
            kd = al(4 * KB, BF16)
            vT = al(4 * KB, BF16)
            vtok = al(2 * KB, BF16).rearrange("p (j d) -> p j d", j=16)
            rC = al(8 * KB, F32)
            rS = al(8 * KB, F32)
            t1 = al(2 * KB, F32)
            t2 = al(2 * KB, F32)
            ev = [al(2 * KB, F32) for _ in range(3)]
            PT = [al(512, BF16) for _ in range(4)]
            assert o <= T_OFF + T_SIZE
            if g == 0:
                DMA("sp", rC, ropeC_d[0], "rope", (), ["rope"])
                DMA("sp", rS, ropeC_d[1], "rope", (), ["rope"])
            wq, wqk = wload(w_odQ[oi][g], (16, 512))
            wkv, wkvk = wload(w_odKV[oi][g], (16, 192))
            for ci in range(4):
                c = g * 4 + ci
                def evq(tb, bank, bkey, ci=ci, c=c):
                    ACT(qc[ci][:, tb * TB:(tb + 1) * TB], bank, AF.Identity, [bkey, "sms"], [("qc", ci, tb)],
                        bias=sms[:, SM_BQ + oi * 16 + c:SM_BQ + oi * 16 + c + 1])
                    rope(qc[ci], ("qc", ci, tb), tb, permC, 128, rC, rS, t1, t2)
                project(wq, wqk, ci * 128, evq)
            def evk(tb, bank, bkey):
                ACT(kd[:, tb * TB:(tb + 1) * TB], bank, AF.Identity, [bkey, "sms"], ["kd"],
                    bias=sms[:, SM_BK + oi * 4 + g:SM_BK + oi * 4 + g + 1])
                rope(kd, "kd", tb, permC, 128, rC, rS, t1, t2)
            project(wkv, wkvk, 0, evk)
            for tb in range(NTB):
                bank, bkey = nextbank()
                for kc in range(NKC):
                    MM(bank[0:64, :], wkv[:, kc, 128:192], HB[:, kc, tb * TB:(tb + 1) * TB], kc == 0, kc == NKC - 1,
                       [wkvk, ("HB", tb)], [bkey])
                ACT(vT[0:64, tb * TB:(tb + 1) * TB], bank[0:64, :], AF.Copy, [bkey], ["vT"])
            for g4 in range(4):
                bank, bkey = nextbank()
                for j in range(4):
                    tt = g4 * 4 + j
                    MM(bank[:, j * 64:(j + 1) * 64], vT[0:64, tt * 128:(tt + 1) * 128], identb[0:64, 0:64], True, True,
                       ["vT", "csb"], [bkey])
                CP("dve", vtok[:, g4 * 4:g4 * 4 + 4, :], bank[:, 0:256].rearrange("p (j d) -> p j d", j=4), [bkey], ["vtok"])
            scale = 64.0 ** -0.5
            cnt = 0
            blocks = []
            for ci in range(4):
                c = g * 4 + ci
                for g4 in range(4):
                    grp = ci * 4 + g4
                    pob, pok = ps[6 + grp % 2][:], ("ps", 6 + grp % 2)
                    plb, plk = nextbank()
                    for hh in range(2):
                        p0 = 64 * hh
                        for j in range(4):
                            qb = g4 * 4 + j
                            hp = qb > 0
                            cur = slice(qb * 128, (qb + 1) * 128)
                            prv = slice((qb - 1) * 128, qb * 128)
                            sc_, pv_ = attn_block(cnt, kd[p0:p0 + 64, cur], kd[p0:p0 + 64, prv] if hp else None,
                                                  qc[ci][p0:p0 + 64, cur], vtok[:, qb, :], vtok[:, qb - 1, :] if hp else None,
                                                  maskC, scale, PT, pob[p0:p0 + 64, j * 128:(j + 1) * 128],
                                                  plb[p0:p0 + 64, j * 128:(j + 1) * 128], (pok, plk),
                                                  [("qc", ci, g4), "kd", "vtok"], onesb[:, 0:64], None)
                            cnt += 1
                            post = None
                            if hh == 1 and j == 3:
                                def post(ci=ci, c=c, g4=g4, pob=pob, plb=plb, pok=pok, plk=plk):
                                    blk = slice(g4 * 512, (g4 + 1) * 512)
                                    bvc = sms[:, SM_BV + oi * 4 + g:SM_BV + oi * 4 + g + 1]
                                    esc = esink[:, oi * 16 + c:oi * 16 + c + 1]
                                    ACT(ev[0], plb, AF.Identity, [plk, "misc"], ["ev0"], bias=esc)
                                    RECIP(ev[0], ev[0], ["ev0"], ["ev0"])
                                    ACT(ev[1], plb, AF.Copy, [plk, "sms"], ["ev1"], scale=bvc)
                                    TT("dve", ev[2], ev[1], pob, ALU.add, ["ev1", pok], ["ev2"])
                                    TT("dve", qc[ci][:, blk], ev[2], ev[0], ALU.mult, ["ev2", "ev0"], [("qc", ci, g4)])
                                    if g4 == 3:
                                        DMA("sp", MIXW[:, :, c, :], qc[ci].rearrange("p (tb t) -> p tb t", tb=NTB), "mixst",
                                            [("qc", ci, t_) for t_ in range(4)], [("MIXD", c)])
                            blocks.append((sc_, pv_, post))
            attn_pipeline(blocks)

        def phase_out(li, spec, last):
            l = spec["l"]
            mixer = spec["mixer"]
            hid = view(HB_OFF, 32 * KB, BF16).rearrange("p (f t) -> p f t", f=32)
            xb = view(HB_OFF + 32 * KB, 32 * KB, F32).rearrange("p (kc t) -> p kc t", kc=NKC)
            mx = view(T_OFF, 16 * KB, BF16).rearrange("p (kc t) -> p kc t", kc=NKC)
            hbb = view(T_OFF + 16 * KB, 16 * KB, BF16).rearrange("p (kc t) -> p kc t", kc=NKC)
            o = T_OFF + 32 * KB
            tv = {"sq": [view(o + i * 2 * KB, 2 * KB, F32) for i in range(2)],
                  "rt": view(o + 4 * KB, 2 * KB, F32), "rr": view(o + 6 * KB, 2 * KB, F32)}
            rl = [view(o + 8 * KB + i * 2 * KB, 2 * KB, F32) for i in range(2)]
            src = xT if li == 0 else XS
            dst = outT if (last and not final) else XS
            xbf = xb.rearrange("p kc t -> p (kc t)")
            for tb in range(NTB):
                blk = slice(tb * TB, (tb + 1) * TB)
                DMA("sp", xbf, src[tb], "xb", [("XS", tb)], ["xb"])
                if mixer is not None:
                    DMA("sp", mx.rearrange("p kc t -> p (kc t)"), MIXD[tb].rearrange("p c t -> p (c t)"), "mx",
                        [("MIXD", c) for c in range(16)], ["mx"])
                    wsrc = w_evO[l // 2] if mixer == "even" else w_odO[l // 2]
                    for mg in range(4):
                        wv, wk = wload(wsrc[mg], (16, 512))
                        for mi in range(4):
                            m = mg * 4 + mi
                            bank, bkey = nextbank()
                            for kc in range(NKC):
                                MM(bank, wv[:, kc, mi * 128:(mi + 1) * 128], mx[:, kc, :], kc == 0, kc == NKC - 1,
                                   [wk, "mx"], [bkey])
                            if mixer == "odd":
                                r_, rk = rl[m % 2], ("rl", m % 2)
                                ACT(r_, bank, AF.Identity, [bkey, "sms"], [rk],
                                    bias=sms[:, SM_BO + (l // 2) * 16 + m:SM_BO + (l // 2) * 16 + m + 1])
                                TT("dve", xb[:, m, :], r_, xb[:, m, :], ALU.add, [rk, "xb"], ["xb"])
                            else:
                                TT("dve", xb[:, m, :], bank, xb[:, m, :], ALU.add, [bkey, "xb"], ["xb"])
                dbg = cfg.get("dbg", 0)
                if spec["mlp"] and dbg == 2:
                    norm_block(xb, "xb", (4 + l) * 16, hbb, "hbb", tv)
                if spec["mlp"] and dbg != 1 and dbg != 2:
                    norm_block(xb, "xb", (4 + l) * 16, hbb, "hbb", tv)
                    for half in range(2):
                        for fg in range(8):
                            wv, wk = wload(w1p[l][half * 8 + fg], (16, 512))
                            for fi in range(4):
                                f = fg * 4 + fi
                                bank, bkey = nextbank()
                                for kc in range(NKC):
                                    MM(bank, wv[:, kc, fi * 128:(fi + 1) * 128], hbb[:, kc, :], kc == 0, kc == NKC - 1,
                                       [wk, "hbb"], [bkey])
                                r_, rk = rl[f % 2], ("rl", f % 2)
                                ACT(r_, bank, AF.Relu, [bkey], [rk])
                                TT(alt(), hid[:, f, :], r_, r_, ALU.mult, [rk], [("hid", f)])
                        for mp in range(8 if dbg != 3 else 0):
                            wv, wk = wload(w2p[l][half, mp], (32, 256))
                            for mi in range(2):
                                m = mp * 2 + mi
                                bank, bkey = nextbank()
                                for f in range(32):
                                    MM(bank, wv[:, f, mi * 128:(mi + 1) * 128], hid[:, f, :], f == 0, f == 31,
                                       [wk, ("hid", f)], [bkey])
                                TT("dve", xb[:, m, :], bank, xb[:, m, :], ALU.add, [bkey, "xb"], ["xb"])
                if last and final:
                    norm_block(xb, "xb", 8 * 16, xb, "xb", tv)
                    DMA("sp", outT[tb], xbf, "xst", ["xb"], [("OUT", tb)])
                else:
                    DMA("sp", dst[tb], xbf, "xst", ["xb"], [("XS", tb)] if dst is XS else [("OUT", tb)])

        for li, spec in enumerate(layers):
            l = spec["l"]
            last = li == len(layers) - 1
            if spec["mixer"] is not None:
                phase_norm(li, l)
                S.barrier()
                if spec["mixer"] == "even":
                    for h in spec.get("heads", range(8)):
                        if spec.get("A", True):
                            even_A(l // 2, h)
                            S.barrier()
                        if spec.get("B", True):
                            even_B(l // 2, h)
                            S.barrier()
                else:
                    for g in range(4):
                        odd_group(l // 2, g)
                        S.barrier()
            phase_out(li, spec, last)
            S.barrier()
        finals = [S.dkeys["xst"][2]]
        S.emit(finals)
    return nc


def prep_shared(inp):
    sh = {}
    c = _host_consts()
    sh.update(c)
    g = np.concatenate([inp["norm_mix_g"], inp["norm_mlp_g"], inp["final_norm_g"][None, :]], axis=0)
    sh["gains"] = np.ascontiguousarray(g.reshape(9, 16, 128).transpose(2, 0, 1).reshape(128, 144))
    wi = inp["even_w_in"]
    def colsA(h):
        return np.r_[h * 128:(h + 1) * 128, 1024 + h * 128:1024 + (h + 1) * 128, 2048 + h * 128:2048 + (h + 1) * 128]
    def colsB(h):
        return np.r_[3072 + h * 128:3072 + (h + 1) * 128, 6144 + h * 128:6144 + (h + 1) * 128,
                     4096 + h * 128:4096 + (h + 1) * 128, 5120 + h * 128:5120 + (h + 1) * 128]
    A = np.stack([np.stack([wi[e][:, colsA(h)] for h in range(8)]) for e in range(2)])
    sh["w_evA"] = np.ascontiguousarray(A.reshape(2, 8, 16, 128, 384).transpose(0, 1, 3, 2, 4))
    B = np.stack([np.stack([wi[e][:, colsB(h)] for h in range(8)]) for e in range(2)])
    sh["w_evB"] = np.ascontiguousarray(B.reshape(2, 8, 16, 128, 512).transpose(0, 1, 3, 2, 4))
    def slab512(w):
        n = w.shape[0]
        return np.ascontiguousarray(w.reshape(n, 16, 128, 4, 512).transpose(0, 3, 2, 1, 4))
    sh["w_evO"] = slab512(inp["even_w_out"])
    sh["w_odO"] = slab512(inp["odd_w_o"])
    wq = inp["odd_w_qkv"]
    sh["w_odQ"] = slab512(wq[:, :, 0:2048])
    kv = np.stack([np.stack([np.concatenate([wq[o][:, 2048 + g * 64:2048 + (g + 1) * 64],
                                             wq[o][:, 2048 + g * 64:2048 + (g + 1) * 64],
                                             wq[o][:, 2304 + g * 64:2304 + (g + 1) * 64]], axis=1) for g in range(4)])
                   for o in range(2)])
    sh["w_odKV"] = np.ascontiguousarray(kv.reshape(2, 4, 16, 128, 192).transpose(0, 1, 3, 2, 4))
    w1 = inp["mlp_w1"]
    sh["w1p"] = np.ascontiguousarray(w1.reshape(4, 16, 128, 16, 512).transpose(0, 3, 2, 1, 4))
    w2 = inp["mlp_w2"]
    sh["w2p"] = np.ascontiguousarray(w2.reshape(4, 2, 32, 128, 8, 256).transpose(0, 1, 4, 3, 2, 5))
    sm = np.zeros((128, 256), np.float32)
    lb = inp["hgrn_lb_raw"].reshape(2, 8, 128)
    sm[:, 0:16] = lb.transpose(2, 0, 1).reshape(128, 16)
    sm[:, 16:18] = inp["hgrn_norm_g"].T
    bq = inp["odd_b_qkv"]
    sm[:, 18:50] = bq[:, 0:2048].reshape(2, 16, 128).transpose(2, 0, 1).reshape(128, 32)
    bk = bq[:, 2048:2304].reshape(2, 4, 64)
    sm[:, 50:58] = np.concatenate([bk, bk], axis=2).transpose(2, 0, 1).reshape(128, 8)
    bv = bq[:, 2304:2560].reshape(2, 4, 64)
    sm[:, 58:66] = np.concatenate([bv, bv], axis=2).transpose(2, 0, 1).reshape(128, 8)
    sk = inp["odd_sinks"].reshape(2, 16, 2)
    sm[:, 66:98] = np.repeat(sk, 64, axis=2).transpose(2, 0, 1).reshape(128, 32)
    sm[:, 98:130] = inp["odd_b_o"].reshape(2, 16, 128).transpose(2, 0, 1).reshape(128, 32)
    sh["smalls"] = sm
    return sh


FULL_CFG = {"layers": [{"l": 0, "mixer": "even", "mlp": True}, {"l": 1, "mixer": "odd", "mlp": True},
                       {"l": 2, "mixer": "even", "mlp": True}, {"l": 3, "mixer": "odd", "mlp": True}], "final": True}


def run(inputs, cfg, ncores=8):
    inp = {k: np.asarray(v) for k, v in inputs.items()}
    sh = prep_shared(inp)
    nc = build(cfg)
    names = set()
    for alloc in nc.allocations:
        try:
            if alloc.kind == "ExternalInput":
                names.add(alloc.memorylocations[0].name)
        except Exception:
            pass
    shared = {}
    for k, v in sh.items():
        if k in names:
            shared[k] = v
        elif k.startswith("w"):
            for i in range(v.shape[0]):
                if (k + str(i)) in names:
                    shared[k + str(i)] = np.ascontiguousarray(v[i])
    x = inp["x"]
    in_maps = []
    for b in range(ncores):
        m = dict(shared)
        m["xT"] = np.ascontiguousarray(x[b].T.reshape(NKC, 128, NTB, TB).transpose(2, 1, 0, 3)).reshape(NTB, 128, NKC * TB)
        in_maps.append(m)
    res = run_bass_kernel_spmd(nc, in_maps, core_ids=list(range(ncores)))
    outs = []
    for r in res.results:
        o = np.asarray(r["outT"]).reshape(NTB, 128, NKC, TB).transpose(2, 1, 0, 3).reshape(D, SEQ)
        outs.append(np.ascontiguousarray(o.T))
    return np.stack(outs).astype(np.float32)


def kernel(**inputs):
    return run(inputs, FULL_CFG, 8)
```

```python
import contextlib
import numpy as np
import concourse.bass as bass
import concourse.mybir as mybir
from concourse.bass_utils import run_bass_kernel_spmd

F32 = mybir.dt.float32
BF16 = mybir.dt.bfloat16
AF = mybir.ActivationFunctionType
ALU = mybir.AluOpType

D = 2048
SEQ = 2048
NKC = 16
TB = 512
NTB = 4
EPS = 1e-5
NEG = -30000.0
SAME_ENGINE_SYNC = True


class _Res:
    __slots__ = ("w", "r", "rd")

    def __init__(self):
        self.w = None
        self.r = {}
        self.rd = []


class _Op:
    __slots__ = ("eng", "fn", "deps", "sig", "sem", "val", "dma", "idx")


class Sched:
    ENG = ("pe", "act", "dve", "pool", "sp")

    def __init__(self, nc):
        self.nc = nc
        self.ops = {e: [] for e in self.ENG}
        self.res = {}
        self.dkeys = {}
        self.pending = {e: [] for e in self.ENG}
        self.last = {e: None for e in self.ENG}
        self.n = 0

    def R(self, key):
        r = self.res.get(key)
        if r is None:
            r = self.res[key] = _Res()
        return r

    def _mk(self, eng, fn, reads, writes, dma):
        op = _Op()
        op.eng, op.fn, op.sig, op.sem, op.val, op.dma = eng, fn, False, None, 0, dma
        op.idx = self.n
        self.n += 1
        deps = {}
        for k in reads:
            r = self.R(k)
            if r.w is not None:
                deps[id(r.w)] = r.w
        for k in writes:
            r = self.R(k)
            if r.w is not None:
                deps[id(r.w)] = r.w
            for o in r.r.values():
                deps[id(o)] = o
            for o in r.rd:
                deps[id(o)] = o
        for k in reads:
            r = self.R(k)
            if dma:
                r.rd.append(op)
            else:
                r.r[eng] = op
        for k in writes:
            r = self.R(k)
            r.w = op
            r.r = {}
            r.rd = []
        dl = []
        for d in deps.values():
            if d is op:
                continue
            if d.dma or dma or d.eng != eng:
                dl.append(d)
            elif SAME_ENGINE_SYNC and eng != "pe":
                dl.append(d)
        if self.pending[eng] and not (dma and eng == "pool"):
            have = set(id(d) for d in dl)
            for d in self.pending[eng]:
                if id(d) not in have and d is not op:
                    dl.append(d)
            self.pending[eng] = []
        op.deps = dl
        for d in dl:
            d.sig = True
        self.ops[eng].append(op)
        if not dma:
            self.last[eng] = op
        return op

    def add(self, eng, fn, reads=(), writes=()):
        return self._mk(eng, fn, reads, writes, False)

    def dma(self, eng, fn, ndma, key, reads=(), writes=()):
        op = self._mk(eng, fn, reads, writes, True)
        ent = self.dkeys.get(key)
        if ent is None:
            ent = self.dkeys[key] = [None, 0, None]
        if ent[2] is not None and all(d is not ent[2] for d in op.deps):
            op.deps.append(ent[2])
        ent[1] += 16 * ndma
        ent[2] = op
        op.sem = key
        op.val = ent[1]
        op.sig = True
        return op

    def barrier(self, engines=("pe", "act", "dve", "sp")):
        lasts = [self.last[e] for e in self.ENG if self.last[e] is not None]
        dl = [ent[2] for ent in self.dkeys.values() if ent[2] is not None]
        for e in engines:
            self.pending[e] = [d for d in lasts if d.eng != e] + dl

    def emit(self, final_waits=()):
        nc = self.nc
        with contextlib.ExitStack() as es:
            esems = {e: es.enter_context(nc.semaphore("s_" + e)) for e in self.ENG}
            dsems = {k: es.enter_context(nc.semaphore("d_%d" % i)) for i, k in enumerate(self.dkeys)}
            for e in self.ENG:
                c = 0
                for op in self.ops[e]:
                    if op.dma:
                        op.sem = dsems[op.sem]
                    elif op.sig:
                        c += 1
                        op.sem = esems[e]
                        op.val = c
            block = es.enter_context(nc.Block())

            def run(engh, e, extra=()):
                waited = {}
                for op in self.ops[e]:
                    for d in op.deps:
                        if waited.get(id(d.sem), 0) < d.val:
                            engh.wait_ge(d.sem, d.val)
                            waited[id(d.sem)] = d.val
                    if op.dma:
                        op.fn(engh, op.sem)
                    else:
                        ins = op.fn(engh)
                        if op.sig:
                            ins.then_inc(op.sem, 1)
                for d in extra:
                    if waited.get(id(d.sem), 0) < d.val:
                        engh.wait_ge(d.sem, d.val)
                        waited[id(d.sem)] = d.val

            @block.tensor
            def _(t):
                run(t, "pe")

            @block.scalar
            def _(t):
                run(t, "act")

            @block.vector
            def _(t):
                run(t, "dve")

            @block.gpsimd
            def _(t):
                run(t, "pool")

            @block.sync
            def _(t):
                run(t, "sp", final_waits)


def _host_consts():
    c = {}
    ki = np.arange(128)[:, None]
    qi = np.arange(128)[None, :]
    mA = np.zeros((128, 256), np.float32)
    mA[:, :128] = np.where(qi >= ki, 0.0, NEG)
    mA[:, 128:] = np.where(qi <= ki, 0.0, NEG)
    mC = np.zeros((128, 256), np.float32)
    mC[:, :128] = np.where(qi >= ki, 0.0, NEG)
    mC[:, 128:] = np.where(qi < ki, 0.0, NEG)
    ident = np.eye(128, dtype=np.float32)
    s = np.arange(128)[:, None]
    t = np.arange(128)[None, :]
    same = (s // 64) == (t // 64)
    sl, tl = s % 64, t % 64
    mid = 32
    trimid = np.where(same, (sl <= tl).astype(np.float32) - (sl <= mid).astype(np.float32), 0.0)
    tristart = np.where(same & (sl <= tl), 1.0, 0.0)
    after = np.where(same & (sl > tl), 1.0, 0.0)
    caus = np.where(sl <= tl, 1.0, 0.0)[:, :64]
    caus = np.tile(caus, (1, 4))
    permA = np.zeros((128, 128), np.float32)
    for m in range(16):
        permA[m + 16, m] = 1.0
        permA[m, m + 16] = 1.0
    permC = np.zeros((128, 128), np.float32)
    for base in (0, 64):
        for m in range(8):
            permC[base + m + 8, base + m] = 1.0
            permC[base + m, base + m + 8] = 1.0
    c["cb16"] = np.ascontiguousarray(np.concatenate([mA, mC, ident, caus, permA, permC], axis=1), np.float32)
    c["cf32"] = np.ascontiguousarray(np.concatenate([trimid, tristart, after, ident], axis=1), np.float32)

    def tables(head_dim):
        rot = head_dim // 4
        inv = (1.0 / (np.float32(500000.0) ** (np.arange(0, rot, 2, dtype=np.float32) / np.float32(rot)))).astype(np.float32)
        ang = (np.arange(SEQ, dtype=np.float32)[:, None] * inv[None, :]).astype(np.float32)
        return np.cos(ang).astype(np.float32).T, np.sin(ang).astype(np.float32).T

    ca, sa = tables(128)
    ropeA = np.zeros((2, 128, SEQ), np.float32)
    ropeA[0, :, :] = 1.0
    ropeA[0, 0:16] = ca
    ropeA[0, 16:32] = ca
    ropeA[1, 0:16] = -sa
    ropeA[1, 16:32] = sa
    cc, sc = tables(64)
    ropeC = np.zeros((2, 128, SEQ), np.float32)
    ropeC[0, :, :] = 1.0
    for base in (0, 64):
        ropeC[0, base:base + 8] = cc
        ropeC[0, base + 8:base + 16] = cc
        ropeC[1, base:base + 8] = -sc
        ropeC[1, base + 8:base + 16] = sc
    c["ropeA"] = ropeA
    c["ropeC"] = ropeC
    return c


def build(cfg):
    layers = cfg["layers"]
    final = cfg.get("final", True)
    nc = bass.Bass("TRN2", target_bir_lowering=False)

    def din(name, shape, dt=F32):
        return nc.dram_tensor(name, list(shape), dt, kind="ExternalInput").ap()

    xT = din("xT", [NTB, 128, NKC * TB])
    gains = din("gains", [128, 9 * 16])
    cb16_d = din("cb16", [128, 1152])
    cf32_d = din("cf32", [128, 512])
    ropeA_d = din("ropeA", [2, 128, SEQ])
    ropeC_d = din("ropeC", [2, 128, SEQ])
    used_e = sorted(set(sp["l"] // 2 for sp in layers if sp["mixer"] == "even"))
    used_o = sorted(set(sp["l"] // 2 for sp in layers if sp["mixer"] == "odd"))
    used_m = sorted(set(sp["l"] for sp in layers if sp["mlp"]))
    w_evA = {e: din("w_evA%d" % e, [8, 128, 16, 384]) for e in used_e}
    w_evB = {e: din("w_evB%d" % e, [8, 128, 16, 512]) for e in used_e}
    w_evO = {e: din("w_evO%d" % e, [4, 128, 16, 512]) for e in used_e}
    w_odQ = {o: din("w_odQ%d" % o, [4, 128, 16, 512]) for o in used_o}
    w_odKV = {o: din("w_odKV%d" % o, [4, 128, 16, 192]) for o in used_o}
    w_odO = {o: din("w_odO%d" % o, [4, 128, 16, 512]) for o in used_o}
    w1p = {l: din("w1p%d" % l, [16, 128, 16, 512]) for l in used_m}
    w2p = {l: din("w2p%d" % l, [2, 8, 128, 32, 256]) for l in used_m}
    smalls = din("smalls", [128, 256])
    outT = nc.dram_tensor("outT", [NTB, 128, NKC * TB], F32, kind="ExternalOutput").ap()
    XS = nc.dram_tensor("xs_scr", [NTB, 128, NKC * TB], F32).ap()
    MIXD = nc.dram_tensor("mix_scr", [NTB, 128, NKC, TB], BF16).ap()
    MIXW = MIXD.rearrange("tb p c t -> p tb c t")

    SM_LBRAW = 0
    SM_GNORM = 16
    SM_BQ = 18
    SM_BK = 50
    SM_BV = 58
    SM_SINK = 66
    SM_BO = 98

    KB = 1024
    with contextlib.ExitStack() as es:
        arena = es.enter_context(nc.sbuf_tensor("arena", [128, 196 * KB // 4], F32))
        csb = es.enter_context(nc.sbuf_tensor("csb", [128, 1152], BF16))
        cfs = es.enter_context(nc.sbuf_tensor("cfs", [128, 512], F32))
        gsb = es.enter_context(nc.sbuf_tensor("gsb", [128, 9 * 16], F32))
        sms = es.enter_context(nc.sbuf_tensor("sms", [128, 256], F32))
        misc = es.enter_context(nc.sbuf_tensor("misc", [128, 512], F32))
        onesb = es.enter_context(nc.sbuf_tensor("onesb", [128, 128], BF16))
        ps = [es.enter_context(nc.psum_tensor("ps%d" % i, [128, 512], F32)) for i in range(8)]
        S = Sched(nc)

        def view(off, nbytes, dt):
            assert off % 4 == 0 and nbytes % 4 == 0 and off + nbytes <= 196 * KB, (off, nbytes)
            v = arena[:, off // 4:(off + nbytes) // 4]
            return v if dt == F32 else v.bitcast(dt)

        W_OFF, HB_OFF, T_OFF = 0, 64 * KB, 128 * KB
        T_SIZE = 68 * KB
        wslots = [view(W_OFF + i * 16 * KB, 16 * KB, BF16) for i in range(4)]
        HB = view(HB_OFF, 64 * KB, BF16).rearrange("p (kc t) -> p kc t", kc=NKC)

        maskA = csb[:, 0:256]
        maskC = csb[:, 256:512]
        identb = csb[:, 512:640]
        caus = csb[:, 640:896]
        permA = csb[:, 896:1024]
        permC = csb[:, 1024:1152]
        trimid = cfs[:, 0:128]
        tristart = cfs[:, 128:256]
        after = cfs[:, 256:384]
        identf = cfs[:, 384:512]
        onesf = misc[:, 0:128]
        epsc = misc[:, 128:129]
        lbc = misc[:, 136:144]
        omlc = misc[:, 144:152]
        esink = misc[:, 160:192]
        zcol = misc[:, 200:201]

        def MM(out, lhsT, rhs, start, stop, R, W):
            S.add("pe", lambda e: e.matmul(out, lhsT=lhsT, rhs=rhs, start=start, stop=stop), R, W)

        def ACT(out, in_, func, R, W, **kw):
            S.add("act", lambda e: e.activation(out=out, in_=in_, func=func, **kw), R, W)

        def TT(eng, out, in0, in1, op, R, W):
            S.add(eng, lambda e: e.tensor_tensor(out=out, in0=in0, in1=in1, op=op), R, W)

        def STT(eng, out, in0, scalar, in1, op0, op1, R, W):
            S.add(eng, lambda e: e.scalar_tensor_tensor(out=out, in0=in0, scalar=scalar, in1=in1, op0=op0, op1=op1), R, W)

        def TS(eng, out, in0, s1, s2, op0, op1, R, W):
            S.add(eng, lambda e: e.tensor_scalar(out=out, in0=in0, scalar1=s1, scalar2=s2, op0=op0, op1=op1), R, W)

        def CP(eng, out, in_, R, W):
            S.add(eng, lambda e: e.tensor_copy(out=out, in_=in_), R, W)

        def RECIP(out, in_, R, W):
            S.add("dve", lambda e: e.reciprocal(out=out, in_=in_), R, W)

        def MEMSET(eng, ap, val, W):
            S.add(eng, lambda e: e.memset(ap, val), (), W)

        def DMA(eng, out, in_, key, R, W):
            return S.dma(eng, lambda e, s: e.dma_start(out=out, in_=in_).then_inc(s, 16), 1, key, R, W)

        st = {"bank": 0, "w": 0, "alt": 0}

        def nextbank():
            i = st["bank"] % 3
            st["bank"] += 1
            return ps[i][:], ("ps", i)

        def alt():
            st["alt"] += 1
            return "dve"

        def wload(src, shape):
            i = st["w"] % 4
            st["w"] += 1
            n = int(np.prod(shape))
            v = wslots[i][:, 0:n]
            DMA("pool", v, src.rearrange("p a b -> p (a b)"), ("W", i), (), [("W", i)])
            v = v.rearrange("p (a b) -> p a b", a=shape[0])
            return v, ("W", i)

        DMA("pool", csb[:], cb16_d, "c0", (), ["csb"])
        DMA("sp", cfs[:], cf32_d, "c1", (), ["cfs"])
        DMA("sp", gsb[:], gains, "c2", (), ["gsb"])
        DMA("sp", sms[:], smalls, "c3", (), ["sms"])
        MEMSET("dve", misc[:], 0.0, ["misc"])
        MEMSET("dve", onesf, 1.0, ["misc"])
        MEMSET("dve", epsc, EPS, ["misc"])
        MEMSET("dve", onesb[:], 1.0, ["onesb"])
        TT("dve", lbc, sms[:, SM_LBRAW + 8:SM_LBRAW + 16], sms[:, SM_LBRAW:SM_LBRAW + 8], ALU.subtract, ["sms"], ["misc"])
        ACT(lbc, lbc, AF.Sigmoid, ["misc"], ["misc"])
        TS("dve", omlc, lbc, -1.0, 1.0, ALU.mult, ALU.add, ["misc"], ["misc"])
        ACT(esink, sms[:, SM_SINK:SM_SINK + 32], AF.Exp, ["sms"], ["misc"])
        CONST = ["csb", "cfs", "gsb", "sms", "misc", "onesb"]

        def norm_block(xv, xkey, gcol0, outv, outkey, tv, inplace_f32=False):
            bank, bkey = nextbank()
            for kc in range(NKC):
                sq, sqk = tv["sq"][kc % 2], ("sq", kc % 2)
                ACT(sq, xv[:, kc, :], AF.Square, [xkey], [sqk])
                MM(bank, onesf, sq, kc == 0, kc == NKC - 1, [sqk, "misc"], [bkey])
            ACT(tv["rt"], bank, AF.Sqrt, [bkey, "misc"], ["rt"], bias=epsc, scale=1.0 / D)
            RECIP(tv["rr"], tv["rt"], ["rt"], ["rr"])
            for kc in range(NKC):
                STT("dve", outv[:, kc, :], xv[:, kc, :], gsb[:, gcol0 + kc:gcol0 + kc + 1], tv["rr"],
                    ALU.mult, ALU.mult, [xkey, "rr", "gsb"], [outkey])

        def phase_norm(li, l):
            xb = view(T_OFF, 32 * KB, F32).rearrange("p (kc t) -> p kc t", kc=NKC)
            tv = {"sq": [view(T_OFF + 32 * KB + i * 2 * KB, 2 * KB, F32) for i in range(2)],
                  "rt": view(T_OFF + 36 * KB, 2 * KB, F32), "rr": view(T_OFF + 38 * KB, 2 * KB, F32)}
            src = xT if li == 0 else XS
            for tb in range(NTB):
                DMA("sp", xb.rearrange("p kc t -> p (kc t)"), src[tb], "xn", [("XS", tb)], ["xn"])
                norm_block(xb, "xn", l * 16, HB[:, :, tb * TB:(tb + 1) * TB], ("HB", tb), tv)

        HBK = [("HB", tb) for tb in range(NTB)]

        def project(wv, wk, col0, evac):
            for tb in range(NTB):
                bank, bkey = nextbank()
                for kc in range(NKC):
                    MM(bank, wv[:, kc, col0:col0 + 128], HB[:, kc, tb * TB:(tb + 1) * TB], kc == 0, kc == NKC - 1,
                       [wk, ("HB", tb)], [bkey])
                evac(tb, bank, bkey)

        def rope(dst, dkey, tb, perm_lhsT, nrow, ropeC_t, ropeS_t, t1, t2):
            blk = slice(tb * TB, (tb + 1) * TB)
            bank, bkey = nextbank()
            MM(bank[0:nrow, :], perm_lhsT[:, 0:nrow], dst[:, blk], True, True, [dkey, "csb"], [bkey])
            TT("dve", t1[0:nrow, :], bank[0:nrow, :], ropeS_t[0:nrow, blk], ALU.mult, [bkey, "rope"], ["rt1"])
            TT("dve", t2[0:nrow, :], dst[0:nrow, blk], ropeC_t[0:nrow, blk], ALU.mult, [dkey, "rope"], ["rt2"])
            TT("dve", dst[0:nrow, blk], t1[0:nrow, :], t2[0:nrow, :], ALU.add, ["rt1", "rt2"], [dkey])

        pss = [(ps[3 + i][:, 0:256], ("pss", i)) for i in range(3)]

        def attn_block(cnt, kT_cur, kT_prev, q_ap, v_cur, v_prev, mask, scale, PT, po, pl, prow, R_in, ones_l, si):
            sv, sk = pss[cnt % 3 if si is None else si]
            pt, ptk = PT[cnt % 4], ("PT", cnt % 4)
            hp = kT_prev is not None
            ncol = 256 if hp else 128

            def scores():
                MM(sv[:, 0:ncol], identb, mask[:, 0:ncol], True, False, ["csb"], [sk])
                MM(sv[:, 0:128], kT_cur, q_ap, False, not hp, R_in, [sk])
                if hp:
                    MM(sv[:, 128:256], kT_prev, q_ap, False, True, R_in, [sk])
                ACT(pt[:, 0:ncol], sv[:, 0:ncol], AF.Exp, [sk], [ptk], scale=scale)

            def pv():
                MM(po, v_cur, pt[:, 0:128], True, not hp, [ptk] + R_in, [prow[0]])
                if hp:
                    MM(po, v_prev, pt[:, 128:256], False, True, [ptk] + R_in, [prow[0]])
                MM(pl, ones_l, pt[:, 0:128], True, not hp, [ptk, "onesb"], [prow[1]])
                if hp:
                    MM(pl, ones_l, pt[:, 128:256], False, True, [ptk, "onesb"], [prow[1]])
            return scores, pv

        def attn_pipeline(blocks, skew=2):
            n = len(blocks)
            for i in range(n + skew):
                if i < n:
                    blocks[i][0]()
                k = i - skew
                if k >= 0:
                    blocks[k][1]()
                    if blocks[k][2] is not None:
                        blocks[k][2]()

        def even_A(e, h):
            o = T_OFF
            qT = view(o, 4 * KB, BF16); o += 4 * KB
            kT = view(o, 4 * KB, BF16); o += 4 * KB
            vT = view(o, 4 * KB, BF16); o += 4 * KB
            vtok = []
            for b in range(3):
                vtok.append(view(o, 4 * KB, BF16).rearrange("p (j d) -> p j d", j=16)); o += 4 * KB
            num = view(o, 8 * KB, F32); o += 8 * KB
            lac = view(o, 8 * KB, F32); o += 8 * KB
            PT = [view(o + i * 512, 512, BF16) for i in range(4)]; o += 2 * KB
            rC = view(o, 8 * KB, F32); o += 8 * KB
            rS = view(o, 8 * KB, F32); o += 8 * KB
            t1 = view(o, 2 * KB, F32); o += 2 * KB
            t2 = view(o, 2 * KB, F32); o += 2 * KB
            oa = view(o, 4 * KB, BF16); o += 4 * KB
            assert o <= T_OFF + T_SIZE
            DMA("sp", rC[0:32, :], ropeA_d[0, 0:32, :], "rope", (), ["rope"])
            DMA("sp", rS[0:32, :], ropeA_d[1, 0:32, :], "rope", (), ["rope"])
            wv, wk = wload(w_evA[e][h], (16, 384))
            for ci, (dst, dk) in enumerate(((qT, "qT"), (kT, "kT"))):
                def ev(tb, bank, bkey, dst=dst, dk=dk):
                    ACT(dst[:, tb * TB:(tb + 1) * TB], bank, AF.Copy, [bkey], [dk])
                    rope(dst, dk, tb, permA, 32, rC, rS, t1, t2)
                project(wv, wk, ci * 128, ev)

            def evv(tb, bank, bkey):
                ACT(vT[:, tb * TB:(tb + 1) * TB], bank, AF.Copy, [bkey], ["vT"])
            project(wv, wk, 256, evv)
            branches = (1, 4, 16)

            def tok_slice(dil, r, jb):
                s0 = dil * 128 * jb + r
                return slice(s0, s0 + dil * 127 + 1, dil)

            for b, dil in enumerate(branches):
                nbc = 16 // dil
                for g4 in range(4):
                    bank, bkey = nextbank()
                    for j in range(4):
                        bi = g4 * 4 + j
                        r, jb = bi // nbc, bi % nbc
                        MM(bank[:, j * 128:(j + 1) * 128], vT[:, tok_slice(dil, r, jb)], identb, True, True,
                           ["vT", "csb"], [bkey])
                    CP("dve", vtok[b][:, g4 * 4:g4 * 4 + 4, :], bank.rearrange("p (j d) -> p j d", j=4), [bkey], [("vtok", b)])
            scale = 128.0 ** -0.5
            cnt = 0
            blocks = []
            for b, dil in enumerate(branches):
                nbc = 16 // dil
                for g4 in range(4):
                    grp = b * 4 + g4
                    pob, pok = ps[6 + grp % 2][:], ("ps", 6 + grp % 2)
                    plb, plk = nextbank()
                    for j in range(4):
                        bi = g4 * 4 + j
                        r, jb = bi // nbc, bi % nbc
                        hp = jb > 0
                        sc_, pv_ = attn_block(cnt, kT[:, tok_slice(dil, r, jb)],
                                              kT[:, tok_slice(dil, r, jb - 1)] if hp else None,
                                              qT[:, tok_slice(dil, r, jb)],
                                              vtok[b][:, bi, :], vtok[b][:, bi - 1, :] if hp else None,
                                              maskA, scale, PT, pob[:, j * 128:(j + 1) * 128], plb[:, j * 128:(j + 1) * 128],
                                              (pok, plk), ["qT", "kT", ("vtok", b)], onesb[:], None)
                        cnt += 1
                        post = None
                        if j == 3:
                            def post(b=b, dil=dil, g4=g4, pob=pob, plb=plb, pok=pok, plk=plk):
                                if dil == 1:
                                    dn, dl_ = num[:, g4 * 512:(g4 + 1) * 512], lac[:, g4 * 512:(g4 + 1) * 512]
                                    sn, sl_ = pob, plb
                                elif dil == 4:
                                    dn, dl_ = num[:, g4::4], lac[:, g4::4]
                                    sn, sl_ = pob, plb
                                else:
                                    dn = num.rearrange("d (p r) -> d r p", r=16)[:, g4 * 4:g4 * 4 + 4, :]
                                    dl_ = lac.rearrange("d (p r) -> d r p", r=16)[:, g4 * 4:g4 * 4 + 4, :]
                                    sn = pob.rearrange("d (a p) -> d a p", a=4)
                                    sl_ = plb.rearrange("d (a p) -> d a p", a=4)
                                if b == 0:
                                    CP("dve", dn, sn, [pok], ["num"])
                                    ACT(dl_, sl_, AF.Copy, [plk], ["lac"])
                                else:
                                    TT("dve", dn, dn, sn, ALU.add, [pok, "num"], ["num"])
                                    TT("dve", dl_, dl_, sl_, ALU.add, [plk, "lac"], ["lac"])
                        blocks.append((sc_, pv_, post))
            attn_pipeline(blocks)
            RECIP(lac, lac, ["lac"], ["lac"])
            TT("dve", oa, num, lac, ALU.mult, ["num", "lac"], ["oa"])
            DMA("sp", MIXW[:, :, h, :], oa.rearrange("p (tb t) -> p tb t", tb=NTB), "mixst", ["oa"], [("MIXD", h)])

        def even_B(e, h):
            o = T_OFF
            def al(n, dt):
                nonlocal o
                v = view(o, n, dt)
                o += n
                return v
            qs = al(8 * KB, F32)
            gsil = al(4 * KB, BF16)
            g_reg = o
            gT = al(8 * KB, F32)
            Sb = view(g_reg, 8 * KB, BF16).rearrange("p (c v) -> p c v", c=32)
            kk_reg = o
            kkT = al(4 * KB, BF16)
            ob = view(kk_reg, 4 * KB, BF16)
            i_reg = o
            iT = al(4 * KB, BF16)
            khat = view(i_reg, 4 * KB, BF16).rearrange("p (j k) -> p j k", j=16)
            gtok = al(8 * KB, F32).rearrange("p (j k) -> p j k", j=16)
            kktok = al(4 * KB, BF16).rearrange("p (j k) -> p j k", j=16)
            itok = al(4 * KB, BF16).rearrange("p (j k) -> p j k", j=16)
            qtil = al(4 * KB, BF16)
            ktil = al(4 * KB, BF16)
            qd = al(4 * KB, BF16)
            tmp = [al(2 * KB, F32) for _ in range(4)]
            Sst = al(512, F32)
            Sst1 = al(512, F32)
            dec = al(128, F32)
            am = [al(512, BF16) for _ in range(2)]
            assert o <= T_OFF + T_SIZE, o - T_OFF
            KG, KK, KI = "B_g", "B_kk", "B_i"
            wv, wk = wload(w_evB[e][h], (16, 512))
            def ev_q(tb, bank, bkey):
                ACT(qs[:, tb * TB:(tb + 1) * TB], bank, AF.Silu, [bkey], ["qs"])
            project(wv, wk, 0, ev_q)
            def ev_g(tb, bank, bkey):
                ACT(gsil[:, tb * TB:(tb + 1) * TB], bank, AF.Silu, [bkey], ["gsil"])
            project(wv, wk, 128, ev_g)
            def ev_f(tb, bank, bkey):
                ACT(gT[:, tb * TB:(tb + 1) * TB], bank, AF.Sigmoid, [bkey], [KG])
            project(wv, wk, 256, ev_f)
            def ev_i(tb, bank, bkey):
                ACT(iT[:, tb * TB:(tb + 1) * TB], bank, AF.Copy, [bkey], [KI])
            project(wv, wk, 384, ev_i)
            bstop = cfg.get("bstop", 99)
            if bstop <= 0:
                return
            if e == 1:
                TS("dve", gT, gT, omlc[:, h:h + 1], lbc[:, h:h + 1], ALU.mult, ALU.add, [KG, "misc"], [KG])
            TS("dve", kkT, gT, -1.0, 1.0, ALU.mult, ALU.add, [KG], [KK])
            ACT(gT, gT, AF.Ln, [KG], [KG])
            if bstop <= 1:
                return
            for g4 in range(4):
                b1, k1 = nextbank()
                b2, k2 = nextbank()
                b3, k3 = nextbank()
                for j in range(4):
                    tt = g4 * 4 + j
                    ts_ = slice(tt * 128, (tt + 1) * 128)
                    MM(b1[:, j * 128:(j + 1) * 128], gT[:, ts_], identf, True, True, [KG, "cfs"], [k1])
                    MM(b2[:, j * 128:(j + 1) * 128], kkT[:, ts_], identb, True, True, [KK, "csb"], [k2])
                    MM(b3[:, j * 128:(j + 1) * 128], iT[:, ts_], identb, True, True, [KI, "csb"], [k3])
                r4 = lambda bk: bk.rearrange("p (j d) -> p j d", j=4)
                CP("dve", gtok[:, g4 * 4:g4 * 4 + 4, :], r4(b1), [k1], ["gtok"])
                ACT(kktok[:, g4 * 4:g4 * 4 + 4, :], r4(b2), AF.Copy, [k2], ["kktok"])
                CP("dve", itok[:, g4 * 4:g4 * 4 + 4, :], r4(b3), [k3], ["itok"])
            if bstop <= 2:
                return
            for g4 in range(4):
                blk = slice(g4 * 512, (g4 + 1) * 512)
                b1, k1 = nextbank()
                for j in range(4):
                    MM(b1[:, j * 128:(j + 1) * 128], gtok[:, g4 * 4 + j, :], trimid, True, True, ["gtok", "cfs"], [k1])
                ACT(tmp[0], b1, AF.Exp, [k1], ["tmp0"])
                ACT(tmp[1], b1, AF.Exp, [k1], ["tmp1"], scale=-1.0)
                TT("dve", qtil[:, blk], qs[:, blk], tmp[0], ALU.mult, ["qs", "tmp0"], ["qtil"])
                TT("dve", ktil[:, blk], kkT[:, blk], tmp[1], ALU.mult, [KK, "tmp1"], ["ktil"])
                b2, k2 = nextbank()
                for j in range(4):
                    MM(b2[:, j * 128:(j + 1) * 128], gtok[:, g4 * 4 + j, :], tristart, True, True, ["gtok", "cfs"], [k2])
                ACT(tmp[2], b2, AF.Exp, [k2], ["tmp2"])
                TT("dve", qd[:, blk], qs[:, blk], tmp[2], ALU.mult, ["qs", "tmp2"], ["qd"])
                CP("dve", dec[:, g4 * 8:(g4 + 1) * 8], tmp[2][:, 63::64], ["tmp2"], ["dec"])
            for g4 in range(4):
                b3, k3 = nextbank()
                for j in range(4):
                    MM(b3[:, j * 128:(j + 1) * 128], after, gtok[:, g4 * 4 + j, :], True, True, ["gtok", "cfs"], [k3])
                ACT(tmp[3], b3, AF.Exp, [k3], ["tmp3"])
                TT("dve", khat[:, g4 * 4:g4 * 4 + 4, :], kktok[:, g4 * 4:g4 * 4 + 4, :],
                   tmp[3].rearrange("p (j d) -> p j d", j=4), ALU.mult, ["kktok", "tmp3"], [KI])
            if bstop <= 3:
                return
            S2 = [Sst, Sst1]
            MEMSET("dve", S2[0], 0.0, [("Sst", 0)])
            MEMSET("dve", Sb[:, 0, :], 0.0, [KG])
            for g4 in range(4):
                bks = [nextbank(), nextbank()]
                for jj in range(4):
                    for par in range(2):
                        c = g4 * 8 + jj * 2 + par
                        tt, p0 = c // 2, 64 * par
                        MM(bks[par][0][:, jj * 128:(jj + 1) * 128], khat[p0:p0 + 64, tt, :], itok[p0:p0 + 64, tt, :],
                           True, True, [KI, "itok"], [bks[par][1]])
                for par in range(2):
                    ACT(tmp[par], bks[par][0], AF.Copy, [bks[par][1]], ["tmp%d" % par])
                for jj in range(4 if cfg.get("b4", 0) != 1 else 0):
                    for par in range(2):
                        c = g4 * 8 + jj * 2 + par
                        if c == 31:
                            break
                        STT("dve", S2[(c + 1) % 2], S2[c % 2], dec[:, c:c + 1], tmp[par][:, jj * 128:(jj + 1) * 128],
                            ALU.mult, ALU.add, [("Sst", c % 2), "dec", "tmp%d" % par], [("Sst", (c + 1) % 2)])
                        ACT(Sb[:, c + 1, :], S2[(c + 1) % 2], AF.Copy, [("Sst", (c + 1) % 2)], [KG])
            if bstop <= 4:
                return
            cK = 128.0 ** -0.5

            def stA(hf):
                g4, half = hf // 2, hf % 2
                sv, sk = pss[hf % 3]
                amv, amk = am[half], ("am", half)
                for j in range(4):
                    c = g4 * 8 + half * 4 + j
                    p0 = 64 * (c % 2)
                    cs = slice(c * 64, (c + 1) * 64)
                    MM(sv[p0:p0 + 64, (j // 2) * 64:(j // 2) * 64 + 64], ktil[:, cs], qtil[:, cs], True, True,
                       ["ktil", "qtil"], [sk])
                TT("dve", amv[:, 0:128], sv[:, 0:128], caus[:, 0:128], ALU.mult, [sk, "csb"], [amk])

            def stO(hf):
                g4, half = hf // 2, hf % 2
                amv, amk = am[half], ("am", half)
                for j in range(4):
                    c = g4 * 8 + half * 4 + j
                    tt, p0 = c // 2, 64 * (c % 2)
                    cs = slice(c * 64, (c + 1) * 64)
                    par = c % 2
                    oc = ((half * 4 + j) // 2) * 64
                    pb, pk = ps[6 + par], ("ps", 6 + par)
                    MM(pb[:, oc:oc + 64], itok[p0:p0 + 64, tt, :], amv[p0:p0 + 64, (j // 2) * 64:(j // 2) * 64 + 64],
                       True, False, ["itok", amk], [pk])
                    MM(pb[:, oc:oc + 64], Sb[:, c, :], qd[:, cs], False, True, [KG, "qd"], [pk])

            def stN1(g4):
                for par in range(2):
                    srcv = ps[6 + par][:, 0:256].rearrange("p (i t) -> p i t", t=64)
                    d0 = tmp[0].rearrange("p (i two t) -> p i two t", two=2, t=64)[:, :, par, :]
                    d1 = tmp[1].rearrange("p (i two t) -> p i two t", two=2, t=64)[:, :, par, :]
                    ACT(d0, srcv, AF.Square, [("ps", 6 + par)], ["tmp0"], scale=cK)
                    ACT(d1, srcv, AF.Copy, [("ps", 6 + par)], ["tmp1"], scale=cK)

            def stN2(g4):
                blk = slice(g4 * 512, (g4 + 1) * 512)
                bank, bkey = nextbank()
                MM(bank, onesf, tmp[0], True, True, ["tmp0", "misc"], [bkey])
                ACT(tmp[2], bank, AF.Sqrt, [bkey, "misc"], ["tmp2"], bias=epsc, scale=1.0 / 128.0)
                RECIP(tmp[2], tmp[2], ["tmp2"], ["tmp2"])
                STT("dve", tmp[3], tmp[1], sms[:, SM_GNORM + e:SM_GNORM + e + 1], tmp[2], ALU.mult, ALU.mult,
                    ["tmp1", "tmp2", "sms"], ["tmp3"])
                TT("dve", ob[:, blk], tmp[3], gsil[:, blk], ALU.mult, ["tmp3", "gsil"], [KK])

            stA(0)
            stA(1)
            stO(0)
            for g4 in range(4):
                if g4 < 3:
                    stA(2 * g4 + 2)
                stO(2 * g4 + 1)
                stN1(g4)
                if g4 < 3:
                    stA(2 * g4 + 3)
                    stO(2 * g4 + 2)
                stN2(g4)
            DMA("sp", MIXW[:, :, 8 + h, :], ob.rearrange("p (tb t) -> p tb t", tb=NTB), "mixst", [KK], [("MIXD", 8 + h)])

        def odd_group(oi, g):
            o = T_OFF
            def al(n, dt):
                nonlocal o
                v = view(o, n, dt)
                o += n
                return v
            qc = [al(4 * KB, BF16) for _ in range(4)]
            kd = al(4 * KB, BF16)
            vT = al(4 * KB, BF16)
            vtok = al(2 * KB, BF16).rearrange("p (j d) -> p j d", j=16)
            rC = al(8 * KB, F32)
            rS = al(8 * KB, F32)
            t1 = al(2 * KB, F32)
            t2 = al(2 * KB, F32)
            ev = [al(2 * KB, F32) for _ in range(3)]
            PT = [al(512, BF16) for _ in range(4)]
            assert o <= T_OFF + T_SIZE
            if g == 0:
                DMA("sp", rC, ropeC_d[0], "rope", (), ["rope"])
                DMA("sp", rS, ropeC_d[1], "rope", (), ["rope"])
            wq, wqk = wload(w_odQ[oi][g], (16, 512))
            wkv, wkvk = wload(w_odKV[oi][g], (16, 192))
            for ci in range(4):
                c = g * 4 + ci
                def evq(tb, bank, bkey, ci=ci, c=c):
                    ACT(qc[ci][:, tb * TB:(tb + 1) * TB], bank, AF.Identity, [bkey, "sms"], [("qc", ci, tb)],
                        bias=sms[:, SM_BQ + oi * 16 + c:SM_BQ + oi * 16 + c + 1])
                    rope(qc[ci], ("qc", ci, tb), tb, permC, 128, rC, rS, t1, t2)
                project(wq, wqk, ci * 128, evq)
            def evk(tb, bank, bkey):
                ACT(kd[:, tb * TB:(tb + 1) * TB], bank, AF.Identity, [bkey, "sms"], ["kd"],
                    bias=sms[:, SM_BK + oi * 4 + g:SM_BK + oi * 4 + g + 1])
                rope(kd, "kd", tb, permC, 128, rC, rS, t1, t2)
            project(wkv, wkvk, 0, evk)
            for tb in range(NTB):
                bank, bkey = nextbank()
                for kc in range(NKC):
                    MM(bank[0:64, :], wkv[:, kc, 128:192], HB[:, kc, tb * TB:(tb + 1) * TB], kc == 0, kc == NKC - 1,
                       [wkvk, ("HB", tb)], [bkey])
                ACT(vT[0:64, tb * TB:(tb + 1) * TB], bank[0:64, :], AF.Copy, [bkey], ["vT"])
            for g4 in range(4):
                bank, bkey = nextbank()
                for j in range(4):
                    tt = g4 * 4 + j
                    MM(bank[:, j * 64:(j + 1) * 64], vT[0:64, tt * 128:(tt + 1) * 128], identb[0:64, 0:64], True, True,
                       ["vT", "csb"], [bkey])
                CP("dve", vtok[:, g4 * 4:g4 * 4 + 4, :], bank[:, 0:256].rearrange("p (j d) -> p j d", j=4), [bkey], ["vtok"])
            scale = 64.0 ** -0.5
            cnt = 0
            blocks = []
            for ci in range(4):
                c = g * 4 + ci
                for g4 in range(4):
                    grp = ci * 4 + g4
                    pob, pok = ps[6 + grp % 2][:], ("ps", 6 + grp % 2)
                    plb, plk = nextbank()
                    for hh in range(2):
                        p0 = 64 * hh
                        for j in range(4):
                            qb = g4 * 4 + j
                            hp = qb > 0
                            cur = slice(qb * 128, (qb + 1) * 128)
                            prv = slice((qb - 1) * 128, qb * 128)
                            sc_, pv_ = attn_block(cnt, kd[p0:p0 + 64, cur], kd[p0:p0 + 64, prv] if hp else None,
                                                  qc[ci][p0:p0 + 64, cur], vtok[:, qb, :], vtok[:, qb - 1, :] if hp else None,
                                                  maskC, scale, PT, pob[p0:p0 + 64, j * 128:(j + 1) * 128],
                                                  plb[p0:p0 + 64, j * 128:(j + 1) * 128], (pok, plk),
                                                  [("qc", ci, g4), "kd", "vtok"], onesb[:, 0:64], None)
                            cnt += 1
                            post = None
                            if hh == 1 and j == 3:
                                def post(ci=ci, c=c, g4=g4, pob=pob, plb=plb, pok=pok, plk=plk):
                                    blk = slice(g4 * 512, (g4 + 1) * 512)
                                    bvc = sms[:, SM_BV + oi * 4 + g:SM_BV + oi * 4 + g + 1]
                                    esc = esink[:, oi * 16 + c:oi * 16 + c + 1]
                                    ACT(ev[0], plb, AF.Identity, [plk, "misc"], ["ev0"], bias=esc)
                                    RECIP(ev[0], ev[0], ["ev0"], ["ev0"])
                                    ACT(ev[1], plb, AF.Copy, [plk, "sms"], ["ev1"], scale=bvc)
                                    TT("dve", ev[2], ev[1], pob, ALU.add, ["ev1", pok], ["ev2"])
                                    TT("dve", qc[ci][:, blk], ev[2], ev[0], ALU.mult, ["ev2", "ev0"], [("qc", ci, g4)])
                                    if g4 == 3:
                                        DMA("sp", MIXW[:, :, c, :], qc[ci].rearrange("p (tb t) -> p tb t", tb=NTB), "mixst",
                                            [("qc", ci, t_) for t_ in range(4)], [("MIXD", c)])
                            blocks.append((sc_, pv_, post))
            attn_pipeline(blocks)

        def phase_out(li, spec, last):
            l = spec["l"]
            mixer = spec["mixer"]
            hid = view(HB_OFF, 32 * KB, BF16).rearrange("p (f t) -> p f t", f=32)
            xb = view(HB_OFF + 32 * KB, 32 * KB, F32).rearrange("p (kc t) -> p kc t", kc=NKC)
            mx = view(T_OFF, 16 * KB, BF16).rearrange("p (kc t) -> p kc t", kc=NKC)
            hbb = view(T_OFF + 16 * KB, 16 * KB, BF16).rearrange("p (kc t) -> p kc t", kc=NKC)
            o = T_OFF + 32 * KB
            tv = {"sq": [view(o + i * 2 * KB, 2 * KB, F32) for i in range(2)],
                  "rt": view(o + 4 * KB, 2 * KB, F32), "rr": view(o + 6 * KB, 2 * KB, F32)}
            rl = [view(o + 8 * KB + i * 2 * KB, 2 * KB, F32) for i in range(2)]
            src = xT if li == 0 else XS
            dst = outT if (last and not final) else XS
            xbf = xb.rearrange("p kc t -> p (kc t)")
            for tb in range(NTB):
                blk = slice(tb * TB, (tb + 1) * TB)
                DMA("sp", xbf, src[tb], "xb", [("XS", tb)], ["xb"])
                if mixer is not None:
                    DMA("sp", mx.rearrange("p kc t -> p (kc t)"), MIXD[tb].rearrange("p c t -> p (c t)"), "mx",
                        [("MIXD", c) for c in range(16)], ["mx"])
                    wsrc = w_evO[l // 2] if mixer == "even" else w_odO[l // 2]
                    for mg in range(4):
                        wv, wk = wload(wsrc[mg], (16, 512))
                        for mi in range(4):
                            m = mg * 4 + mi
                            bank, bkey = nextbank()
                            for kc in range(NKC):
                                MM(bank, wv[:, kc, mi * 128:(mi + 1) * 128], mx[:, kc, :], kc == 0, kc == NKC - 1,
                                   [wk, "mx"], [bkey])
                            if mixer == "odd":
                                r_, rk = rl[m % 2], ("rl", m % 2)
                                ACT(r_, bank, AF.Identity, [bkey, "sms"], [rk],
                                    bias=sms[:, SM_BO + (l // 2) * 16 + m:SM_BO + (l // 2) * 16 + m + 1])
                                TT("dve", xb[:, m, :], r_, xb[:, m, :], ALU.add, [rk, "xb"], ["xb"])
                            else:
                                TT("dve", xb[:, m, :], bank, xb[:, m, :], ALU.add, [bkey, "xb"], ["xb"])
                dbg = cfg.get("dbg", 0)
                if spec["mlp"] and dbg == 2:
                    norm_block(xb, "xb", (4 + l) * 16, hbb, "hbb", tv)
                if spec["mlp"] and dbg != 1 and dbg != 2:
                    norm_block(xb, "xb", (4 + l) * 16, hbb, "hbb", tv)
                    for half in range(2):
                        for fg in range(8):
                            wv, wk = wload(w1p[l][half * 8 + fg], (16, 512))
                            for fi in range(4):
                                f = fg * 4 + fi
                                bank, bkey = nextbank()
                                for kc in range(NKC):
                                    MM(bank, wv[:, kc, fi * 128:(fi + 1) * 128], hbb[:, kc, :], kc == 0, kc == NKC - 1,
                                       [wk, "hbb"], [bkey])
                                r_, rk = rl[f % 2], ("rl", f % 2)
                                ACT(r_, bank, AF.Relu, [bkey], [rk])
                                TT(alt(), hid[:, f, :], r_, r_, ALU.mult, [rk], [("hid", f)])
                        for mp in range(8 if dbg != 3 else 0):
                            wv, wk = wload(w2p[l][half, mp], (32, 256))
                            for mi in range(2):
                                m = mp * 2 + mi
                                bank, bkey = nextbank()
                                for f in range(32):
                                    MM(bank, wv[:, f, mi * 128:(mi + 1) * 128], hid[:, f, :], f == 0, f == 31,
                                       [wk, ("hid", f)], [bkey])
                                TT("dve", xb[:, m, :], bank, xb[:, m, :], ALU.add, [bkey, "xb"], ["xb"])
                if last and final:
                    norm_block(xb, "xb", 8 * 16, xb, "xb", tv)
                    DMA("sp", outT[tb], xbf, "xst", ["xb"], [("OUT", tb)])
                else:
                    DMA("sp", dst[tb], xbf, "xst", ["xb"], [("XS", tb)] if dst is XS else [("OUT", tb)])

        for li, spec in enumerate(layers):
            l = spec["l"]
            last = li == len(layers) - 1
            if spec["mixer"] is not None:
                phase_norm(li, l)
                S.barrier()
                if spec["mixer"] == "even":
                    for h in spec.get("heads", range(8)):
                        if spec.get("A", True):
                            even_A(l // 2, h)
                            S.barrier()
                        if spec.get("B", True):
                            even_B(l // 2, h)
                            S.barrier()
                else:
                    for g in range(4):
                        odd_group(l // 2, g)
                        S.barrier()
            phase_out(li, spec, last)
            S.barrier()
        finals = [S.dkeys["xst"][2]]
        S.emit(finals)
    return nc


def prep_shared(inp):
    sh = {}
    c = _host_consts()
    sh.update(c)
    g = np.concatenate([inp["norm_mix_g"], inp["norm_mlp_g"], inp["final_norm_g"][None, :]], axis=0)
    sh["gains"] = np.ascontiguousarray(g.reshape(9, 16, 128).transpose(2, 0, 1).reshape(128, 144))
    wi = inp["even_w_in"]
    def colsA(h):
        return np.r_[h * 128:(h + 1) * 128, 1024 + h * 128:1024 + (h + 1) * 128, 2048 + h * 128:2048 + (h + 1) * 128]
    def colsB(h):
        return np.r_[3072 + h * 128:3072 + (h + 1) * 128, 6144 + h * 128:6144 + (h + 1) * 128,
                     4096 + h * 128:4096 + (h + 1) * 128, 5120 + h * 128:5120 + (h + 1) * 128]
    A = np.stack([np.stack([wi[e][:, colsA(h)] for h in range(8)]) for e in range(2)])
    sh["w_evA"] = np.ascontiguousarray(A.reshape(2, 8, 16, 128, 384).transpose(0, 1, 3, 2, 4))
    B = np.stack([np.stack([wi[e][:, colsB(h)] for h in range(8)]) for e in range(2)])
    sh["w_evB"] = np.ascontiguousarray(B.reshape(2, 8, 16, 128, 512).transpose(0, 1, 3, 2, 4))
    def slab512(w):
        n = w.shape[0]
        return np.ascontiguousarray(w.reshape(n, 16, 128, 4, 512).transpose(0, 3, 2, 1, 4))
    sh["w_evO"] = slab512(inp["even_w_out"])
    sh["w_odO"] = slab512(inp["odd_w_o"])
    wq = inp["odd_w_qkv"]
    sh["w_odQ"] = slab512(wq[:, :, 0:2048])
    kv = np.stack([np.stack([np.concatenate([wq[o][:, 2048 + g * 64:2048 + (g + 1) * 64],
                                             wq[o][:, 2048 + g * 64:2048 + (g + 1) * 64],
                                             wq[o][:, 2304 + g * 64:2304 + (g + 1) * 64]], axis=1) for g in range(4)])
                   for o in range(2)])
    sh["w_odKV"] = np.ascontiguousarray(kv.reshape(2, 4, 16, 128, 192).transpose(0, 1, 3, 2, 4))
    w1 = inp["mlp_w1"]
    sh["w1p"] = np.ascontiguousarray(w1.reshape(4, 16, 128, 16, 512).transpose(0, 3, 2, 1, 4))
    w2 = inp["mlp_w2"]
    sh["w2p"] = np.ascontiguousarray(w2.reshape(4, 2, 32, 128, 8, 256).transpose(0, 1, 4, 3, 2, 5))
    sm = np.zeros((128, 256), np.float32)
    lb = inp["hgrn_lb_raw"].reshape(2, 8, 128)
    sm[:, 0:16] = lb.transpose(2, 0, 1).reshape(128, 16)
    sm[:, 16:18] = inp["hgrn_norm_g"].T
    bq = inp["odd_b_qkv"]
    sm[:, 18:50] = bq[:, 0:2048].reshape(2, 16, 128).transpose(2, 0, 1).reshape(128, 32)
    bk = bq[:, 2048:2304].reshape(2, 4, 64)
    sm[:, 50:58] = np.concatenate([bk, bk], axis=2).transpose(2, 0, 1).reshape(128, 8)
    bv = bq[:, 2304:2560].reshape(2, 4, 64)
    sm[:, 58:66] = np.concatenate([bv, bv], axis=2).transpose(2, 0, 1).reshape(128, 8)
    sk = inp["odd_sinks"].reshape(2, 16, 2)
    sm[:, 66:98] = np.repeat(sk, 64, axis=2).transpose(2, 0, 1).reshape(128, 32)
    sm[:, 98:130] = inp["odd_b_o"].reshape(2, 16, 128).transpose(2, 0, 1).reshape(128, 32)
    sh["smalls"] = sm
    return sh


FULL_CFG = {"layers": [{"l": 0, "mixer": "even", "mlp": True}, {"l": 1, "mixer": "odd", "mlp": True},
                       {"l": 2, "mixer": "even", "mlp": True}, {"l": 3, "mixer": "odd", "mlp": True}], "final": True}


def run(inputs, cfg, ncores=8):
    inp = {k: np.asarray(v) for k, v in inputs.items()}
    sh = prep_shared(inp)
    nc = build(cfg)
    names = set()
    for alloc in nc.allocations:
        try:
            if alloc.kind == "ExternalInput":
                names.add(alloc.memorylocations[0].name)
        except Exception:
            pass
    shared = {}
    for k, v in sh.items():
        if k in names:
            shared[k] = v
        elif k.startswith("w"):
            for i in range(v.shape[0]):
                if (k + str(i)) in names:
                    shared[k + str(i)] = np.ascontiguousarray(v[i])
    x = inp["x"]
    in_maps = []
    for b in range(ncores):
        m = dict(shared)
        m["xT"] = np.ascontiguousarray(x[b].T.reshape(NKC, 128, NTB, TB).transpose(2, 1, 0, 3)).reshape(NTB, 128, NKC * TB)
        in_maps.append(m)
    res = run_bass_kernel_spmd(nc, in_maps, core_ids=list(range(ncores)))
    outs = []
    for r in res.results:
        o = np.asarray(r["outT"]).reshape(NTB, 128, NKC, TB).transpose(2, 1, 0, 3).reshape(D, SEQ)
        outs.append(np.ascontiguousarray(o.T))
    return np.stack(outs).astype(np.float32)


def kernel(**inputs):
    return run(inputs, FULL_CFG, 8)
```

```python
import contextlib
import numpy as np
import concourse.bass as bass
import concourse.mybir as mybir
from concourse.bass_utils import run_bass_kernel_spmd

F32 = mybir.dt.float32
BF16 = mybir.dt.bfloat16
AF = mybir.ActivationFunctionType
ALU = mybir.AluOpType

D = 2048
SEQ = 2048
NKC = 16
TB = 512
NTB = 4
EPS = 1e-5
NEG = -30000.0
SAME_ENGINE_SYNC = True


class _Res:
    __slots__ = ("w", "r", "rd")

    def __init__(self):
        self.w = None
        self.r = {}
        self.rd = []


class _Op:
    __slots__ = ("eng", "fn", "deps", "sig", "sem", "val", "dma", "idx")


class Sched:
    ENG = ("pe", "act", "dve", "pool", "sp")

    def __init__(self, nc):
        self.nc = nc
        self.ops = {e: [] for e in self.ENG}
        self.res = {}
        self.dkeys = {}
        self.pending = {e: [] for e in self.ENG}
        self.last = {e: None for e in self.ENG}
        self.n = 0

    def R(self, key):
        r = self.res.get(key)
        if r is None:
            r = self.res[key] = _Res()
        return r

    def _mk(self, eng, fn, reads, writes, dma):
        op = _Op()
        op.eng, op.fn, op.sig, op.sem, op.val, op.dma = eng, fn, False, None, 0, dma
        op.idx = self.n
        self.n += 1
        deps = {}
        for k in reads:
            r = self.R(k)
            if r.w is not None:
                deps[id(r.w)] = r.w
        for k in writes:
            r = self.R(k)
            if r.w is not None:
                deps[id(r.w)] = r.w
            for o in r.r.values():
                deps[id(o)] = o
            for o in r.rd:
                deps[id(o)] = o
        for k in reads:
            r = self.R(k)
            if dma:
                r.rd.append(op)
            else:
                r.r[eng] = op
        for k in writes:
            r = self.R(k)
            r.w = op
            r.r = {}
            r.rd = []
        dl = []
        for d in deps.values():
            if d is op:
                continue
            if d.dma or dma or d.eng != eng:
                dl.append(d)
            elif SAME_ENGINE_SYNC and eng != "pe":
                dl.append(d)
        if self.pending[eng] and not (dma and eng == "pool"):
            have = set(id(d) for d in dl)
            for d in self.pending[eng]:
                if id(d) not in have and d is not op:
                    dl.append(d)
            self.pending[eng] = []
        op.deps = dl
        for d in dl:
            d.sig = True
        self.ops[eng].append(op)
        if not dma:
            self.last[eng] = op
        return op

    def add(self, eng, fn, reads=(), writes=()):
        return self._mk(eng, fn, reads, writes, False)

    def dma(self, eng, fn, ndma, key, reads=(), writes=()):
        op = self._mk(eng, fn, reads, writes, True)
        ent = self.dkeys.get(key)
        if ent is None:
            ent = self.dkeys[key] = [None, 0, None]
        if ent[2] is not None and all(d is not ent[2] for d in op.deps):
            op.deps.append(ent[2])
        ent[1] += 16 * ndma
        ent[2] = op
        op.sem = key
        op.val = ent[1]
        op.sig = True
        return op

    def barrier(self, engines=("pe", "act", "dve", "sp")):
        lasts = [self.last[e] for e in self.ENG if self.last[e] is not None]
        dl = [ent[2] for ent in self.dkeys.values() if ent[2] is not None]
        for e in engines:
            self.pending[e] = [d for d in lasts if d.eng != e] + dl

    def emit(self, final_waits=()):
        nc = self.nc
        with contextlib.ExitStack() as es:
            esems = {e: es.enter_context(nc.semaphore("s_" + e)) for e in self.ENG}
            dsems = {k: es.enter_context(nc.semaphore("d_%d" % i)) for i, k in enumerate(self.dkeys)}
            for e in self.ENG:
                c = 0
                for op in self.ops[e]:
                    if op.dma:
                        op.sem = dsems[op.sem]
                    elif op.sig:
                        c += 1
                        op.sem = esems[e]
                        op.val = c
            block = es.enter_context(nc.Block())

            def run(engh, e, extra=()):
                waited = {}
                for op in self.ops[e]:
                    for d in op.deps:
                        if waited.get(id(d.sem), 0) < d.val:
                            engh.wait_ge(d.sem, d.val)
                            waited[id(d.sem)] = d.val
                    if op.dma:
                        op.fn(engh, op.sem)
                    else:
                        ins = op.fn(engh)
                        if op.sig:
                            ins.then_inc(op.sem, 1)
                for d in extra:
                    if waited.get(id(d.sem), 0) < d.val:
                        engh.wait_ge(d.sem, d.val)
                        waited[id(d.sem)] = d.val

            @block.tensor
            def _(t):
                run(t, "pe")

            @block.scalar
            def _(t):
                run(t, "act")

            @block.vector
            def _(t):
                run(t, "dve")

            @block.gpsimd
            def _(t):
                run(t, "pool")

            @block.sync
            def _(t):
                run(t, "sp", final_waits)


def _host_consts():
    c = {}
    ki = np.arange(128)[:, None]
    qi = np.arange(128)[None, :]
    mA = np.zeros((128, 256), np.float32)
    mA[:, :128] = np.where(qi >= ki, 0.0, NEG)
    mA[:, 128:] = np.where(qi <= ki, 0.0, NEG)
    mC = np.zeros((128, 256), np.float32)
    mC[:, :128] = np.where(qi >= ki, 0.0, NEG)
    mC[:, 128:] = np.where(qi < ki, 0.0, NEG)
    ident = np.eye(128, dtype=np.float32)
    s = np.arange(128)[:, None]
    t = np.arange(128)[None, :]
    same = (s // 64) == (t // 64)
    sl, tl = s % 64, t % 64
    mid = 32
    trimid = np.where(same, (sl <= tl).astype(np.float32) - (sl <= mid).astype(np.float32), 0.0)
    tristart = np.where(same & (sl <= tl), 1.0, 0.0)
    after = np.where(same & (sl > tl), 1.0, 0.0)
    caus = np.where(sl <= tl, 1.0, 0.0)[:, :64]
    caus = np.tile(caus, (1, 4))
    permA = np.zeros((128, 128), np.float32)
    for m in range(16):
        permA[m + 16, m] = 1.0
        permA[m, m + 16] = 1.0
    permC = np.zeros((128, 128), np.float32)
    for base in (0, 64):
        for m in range(8):
            permC[base + m + 8, base + m] = 1.0
            permC[base + m, base + m + 8] = 1.0
    c["cb16"] = np.ascontiguousarray(np.concatenate([mA, mC, ident, caus, permA, permC], axis=1), np.float32)
    c["cf32"] = np.ascontiguousarray(np.concatenate([trimid, tristart, after, ident], axis=1), np.float32)

    def tables(head_dim):
        rot = head_dim // 4
        inv = (1.0 / (np.float32(500000.0) ** (np.arange(0, rot, 2, dtype=np.float32) / np.float32(rot)))).astype(np.float32)
        ang = (np.arange(SEQ, dtype=np.float32)[:, None] * inv[None, :]).astype(np.float32)
        return np.cos(ang).astype(np.float32).T, np.sin(ang).astype(np.float32).T

    ca, sa = tables(128)
    ropeA = np.zeros((2, 128, SEQ), np.float32)
    ropeA[0, :, :] = 1.0
    ropeA[0, 0:16] = ca
    ropeA[0, 16:32] = ca
    ropeA[1, 0:16] = -sa
    ropeA[1, 16:32] = sa
    cc, sc = tables(64)
    ropeC = np.zeros((2, 128, SEQ), np.float32)
    ropeC[0, :, :] = 1.0
    for base in (0, 64):
        ropeC[0, base:base + 8] = cc
        ropeC[0, base + 8:base + 16] = cc
        ropeC[1, base:base + 8] = -sc
        ropeC[1, base + 8:base + 16] = sc
    c["ropeA"] = ropeA
    c["ropeC"] = ropeC
    return c


def build(cfg):
    layers = cfg["layers"]
    final = cfg.get("final", True)
    nc = bass.Bass("TRN2", target_bir_lowering=False)

    def din(name, shape, dt=F32):
        return nc.dram_tensor(name, list(shape), dt, kind="ExternalInput").ap()

    xT = din("xT", [NTB, 128, NKC * TB])
    gains = din("gains", [128, 9 * 16])
    cb16_d = din("cb16", [128, 1152])
    cf32_d = din("cf32", [128, 512])
    ropeA_d = din("ropeA", [2, 128, SEQ])
    ropeC_d = din("ropeC", [2, 128, SEQ])
    used_e = sorted(set(sp["l"] // 2 for sp in layers if sp["mixer"] == "even"))
    used_o = sorted(set(sp["l"] // 2 for sp in layers if sp["mixer"] == "odd"))
    used_m = sorted(set(sp["l"] for sp in layers if sp["mlp"]))
    w_evA = {e: din("w_evA%d" % e, [8, 128, 16, 384]) for e in used_e}
    w_evB = {e: din("w_evB%d" % e, [8, 128, 16, 512]) for e in used_e}
    w_evO = {e: din("w_evO%d" % e, [4, 128, 16, 512]) for e in used_e}
    w_odQ = {o: din("w_odQ%d" % o, [4, 128, 16, 512]) for o in used_o}
    w_odKV = {o: din("w_odKV%d" % o, [4, 128, 16, 192]) for o in used_o}
    w_odO = {o: din("w_odO%d" % o, [4, 128, 16, 512]) for o in used_o}
    w1p = {l: din("w1p%d" % l, [16, 128, 16, 512]) for l in used_m}
    w2p = {l: din("w2p%d" % l, [2, 8, 128, 32, 256]) for l in used_m}
    smalls = din("smalls", [128, 256])
    outT = nc.dram_tensor("outT", [NTB, 128, NKC * TB], F32, kind="ExternalOutput").ap()
    XS = nc.dram_tensor("xs_scr", [NTB, 128, NKC * TB], F32).ap()
    MIXD = nc.dram_tensor("mix_scr", [NTB, 128, NKC, TB], BF16).ap()
    MIXW = MIXD.rearrange("tb p c t -> p tb c t")

    SM_LBRAW = 0
    SM_GNORM = 16
    SM_BQ = 18
    SM_BK = 50
    SM_BV = 58
    SM_SINK = 66
    SM_BO = 98

    KB = 1024
    with contextlib.ExitStack() as es:
        arena = es.enter_context(nc.sbuf_tensor("arena", [128, 196 * KB // 4], F32))
        csb = es.enter_context(nc.sbuf_tensor("csb", [128, 1152], BF16))
        cfs = es.enter_context(nc.sbuf_tensor("cfs", [128, 512], F32))
        gsb = es.enter_context(nc.sbuf_tensor("gsb", [128, 9 * 16], F32))
        sms = es.enter_context(nc.sbuf_tensor("sms", [128, 256], F32))
        misc = es.enter_context(nc.sbuf_tensor("misc", [128, 512], F32))
        onesb = es.enter_context(nc.sbuf_tensor("onesb", [128, 128], BF16))
        ps = [es.enter_context(nc.psum_tensor("ps%d" % i, [128, 512], F32)) for i in range(8)]
        S = Sched(nc)

        def view(off, nbytes, dt):
            assert off % 4 == 0 and nbytes % 4 == 0 and off + nbytes <= 196 * KB, (off, nbytes)
            v = arena[:, off // 4:(off + nbytes) // 4]
            return v if dt == F32 else v.bitcast(dt)

        W_OFF, HB_OFF, T_OFF = 0, 64 * KB, 128 * KB
        T_SIZE = 68 * KB
        wslots = [view(W_OFF + i * 16 * KB, 16 * KB, BF16) for i in range(4)]
        HB = view(HB_OFF, 64 * KB, BF16).rearrange("p (kc t) -> p kc t", kc=NKC)

        maskA = csb[:, 0:256]
        maskC = csb[:, 256:512]
        identb = csb[:, 512:640]
        caus = csb[:, 640:896]
        permA = csb[:, 896:1024]
        permC = csb[:, 1024:1152]
        trimid = cfs[:, 0:128]
        tristart = cfs[:, 128:256]
        after = cfs[:, 256:384]
        identf = cfs[:, 384:512]
        onesf = misc[:, 0:128]
        epsc = misc[:, 128:129]
        lbc = misc[:, 136:144]
        omlc = misc[:, 144:152]
        esink = misc[:, 160:192]
        zcol = misc[:, 200:201]

        def MM(out, lhsT, rhs, start, stop, R, W):
            S.add("pe", lambda e: e.matmul(out, lhsT=lhsT, rhs=rhs, start=start, stop=stop), R, W)

        def ACT(out, in_, func, R, W, **kw):
            S.add("act", lambda e: e.activation(out=out, in_=in_, func=func, **kw), R, W)

        def TT(eng, out, in0, in1, op, R, W):
            S.add(eng, lambda e: e.tensor_tensor(out=out, in0=in0, in1=in1, op=op), R, W)

        def STT(eng, out, in0, scalar, in1, op0, op1, R, W):
            S.add(eng, lambda e: e.scalar_tensor_tensor(out=out, in0=in0, scalar=scalar, in1=in1, op0=op0, op1=op1), R, W)

        def TS(eng, out, in0, s1, s2, op0, op1, R, W):
            S.add(eng, lambda e: e.tensor_scalar(out=out, in0=in0, scalar1=s1, scalar2=s2, op0=op0, op1=op1), R, W)

        def CP(eng, out, in_, R, W):
            S.add(eng, lambda e: e.tensor_copy(out=out, in_=in_), R, W)

        def RECIP(out, in_, R, W):
            S.add("dve", lambda e: e.reciprocal(out=out, in_=in_), R, W)

        def MEMSET(eng, ap, val, W):
            S.add(eng, lambda e: e.memset(ap, val), (), W)

        def DMA(eng, out, in_, key, R, W):
            return S.dma(eng, lambda e, s: e.dma_start(out=out, in_=in_).then_inc(s, 16), 1, key, R, W)

        st = {"bank": 0, "w": 0, "alt": 0}

        def nextbank():
            i = st["bank"] % 3
            st["bank"] += 1
            return ps[i][:], ("ps", i)

        def alt():
            st["alt"] += 1
            return "dve"

        def wload(src, shape):
            i = st["w"] % 4
            st["w"] += 1
            n = int(np.prod(shape))
            v = wslots[i][:, 0:n]
            DMA("pool", v, src.rearrange("p a b -> p (a b)"), ("W", i), (), [("W", i)])
            v = v.rearrange("p (a b) -> p a b", a=shape[0])
            return v, ("W", i)

        DMA("pool", csb[:], cb16_d, "c0", (), ["csb"])
        DMA("sp", cfs[:], cf32_d, "c1", (), ["cfs"])
        DMA("sp", gsb[:], gains, "c2", (), ["gsb"])
        DMA("sp", sms[:], smalls, "c3", (), ["sms"])
        MEMSET("dve", misc[:], 0.0, ["misc"])
        MEMSET("dve", onesf, 1.0, ["misc"])
        MEMSET("dve", epsc, EPS, ["misc"])
        MEMSET("dve", onesb[:], 1.0, ["onesb"])
        TT("dve", lbc, sms[:, SM_LBRAW + 8:SM_LBRAW + 16], sms[:, SM_LBRAW:SM_LBRAW + 8], ALU.subtract, ["sms"], ["misc"])
        ACT(lbc, lbc, AF.Sigmoid, ["misc"], ["misc"])
        TS("dve", omlc, lbc, -1.0, 1.0, ALU.mult, ALU.add, ["misc"], ["misc"])
        ACT(esink, sms[:, SM_SINK:SM_SINK + 32], AF.Exp, ["sms"], ["misc"])
        CONST = ["csb", "cfs", "gsb", "sms", "misc", "onesb"]

        def norm_block(xv, xkey, gcol0, outv, outkey, tv, inplace_f32=False):
            bank, bkey = nextbank()
            xkf = xkey if callable(xkey) else (lambda kc: xkey)
            okf = outkey if callable(outkey) else (lambda kc: outkey)
            for kc in range(NKC):
                sq, sqk = tv["sq"][kc % 2], ("sq", kc % 2)
                ACT(sq, xv[:, kc, :], AF.Square, [xkf(kc)], [sqk])
                MM(bank, onesf, sq, kc == 0, kc == NKC - 1, [sqk, "misc"], [bkey])
            ACT(tv["rt"], bank, AF.Sqrt, [bkey, "misc"], ["rt"], bias=epsc, scale=1.0 / D)
            RECIP(tv["rr"], tv["rt"], ["rt"], ["rr"])
            for kc in range(NKC):
                STT("dve", outv[:, kc, :], xv[:, kc, :], gsb[:, gcol0 + kc:gcol0 + kc + 1], tv["rr"],
                    ALU.mult, ALU.mult, [xkf(kc), "rr", "gsb"], [okf(kc)])

        def phase_norm(li, l):
            xb = view(T_OFF, 32 * KB, F32).rearrange("p (kc t) -> p kc t", kc=NKC)
            tv = {"sq": [view(T_OFF + 32 * KB + i * 2 * KB, 2 * KB, F32) for i in range(2)],
                  "rt": view(T_OFF + 36 * KB, 2 * KB, F32), "rr": view(T_OFF + 38 * KB, 2 * KB, F32)}
            src = xT if li == 0 else XS
            for tb in range(NTB):
                DMA("sp", xb.rearrange("p kc t -> p (kc t)"), src[tb], "xn", [("XS", tb)], ["xn"])
                norm_block(xb, "xn", l * 16, HB[:, :, tb * TB:(tb + 1) * TB], ("HB", tb), tv)

        HBK = [("HB", tb) for tb in range(NTB)]

        def project(wv, wk, col0, evac):
            later = None
            for tb in range(NTB):
                bank, bkey = nextbank()
                for kc in range(NKC):
                    MM(bank, wv[:, kc, col0:col0 + 128], HB[:, kc, tb * TB:(tb + 1) * TB], kc == 0, kc == NKC - 1,
                       [wk, ("HB", tb)], [bkey])
                if later is not None:
                    later()
                later = evac(tb, bank, bkey)
            if later is not None:
                later()

        def rope(dst, dkey, tb, perm_lhsT, nrow, ropeC_t, ropeS_t, t1, t2):
            blk = slice(tb * TB, (tb + 1) * TB)
            bank, bkey = nextbank()
            MM(bank[0:nrow, :], perm_lhsT[:, 0:nrow], dst[:, blk], True, True, [dkey, "csb"], [bkey])
            TT("dve", t1[0:nrow, :], bank[0:nrow, :], ropeS_t[0:nrow, blk], ALU.mult, [bkey, "rope"], ["rt1"])
            TT("dve", t2[0:nrow, :], dst[0:nrow, blk], ropeC_t[0:nrow, blk], ALU.mult, [dkey, "rope"], ["rt2"])
            TT("dve", dst[0:nrow, blk], t1[0:nrow, :], t2[0:nrow, :], ALU.add, ["rt1", "rt2"], [dkey])

        pss = [(ps[3 + i][:, 0:256], ("pss", i)) for i in range(3)]

        def attn_block(cnt, kT_cur, kT_prev, q_ap, v_cur, v_prev, mask, scale, PT, po, pl, prow, R_in, ones_l, si):
            sv, sk = pss[cnt % 3 if si is None else si]
            pt, ptk = PT[cnt % 4], ("PT", cnt % 4)
            hp = kT_prev is not None
            ncol = 256 if hp else 128

            def scores():
                MM(sv[:, 0:ncol], identb, mask[:, 0:ncol], True, False, ["csb"], [sk])
                MM(sv[:, 0:128], kT_cur, q_ap, False, not hp, R_in, [sk])
                if hp:
                    MM(sv[:, 128:256], kT_prev, q_ap, False, True, R_in, [sk])
                ACT(pt[:, 0:ncol], sv[:, 0:ncol], AF.Exp, [sk], [ptk], scale=scale)

            def pv():
                MM(po, v_cur, pt[:, 0:128], True, not hp, [ptk] + R_in, [prow[0]])
                if hp:
                    MM(po, v_prev, pt[:, 128:256], False, True, [ptk] + R_in, [prow[0]])
                MM(pl, ones_l, pt[:, 0:128], True, not hp, [ptk, "onesb"], [prow[1]])
                if hp:
                    MM(pl, ones_l, pt[:, 128:256], False, True, [ptk, "onesb"], [prow[1]])
            return scores, pv

        def attn_pipeline(blocks, skew=2):
            n = len(blocks)
            for i in range(n + skew):
                if i < n:
                    blocks[i][0]()
                k = i - skew
                if k >= 0:
                    blocks[k][1]()
                    if blocks[k][2] is not None:
                        blocks[k][2]()

        def even_A(e, h):
            o = T_OFF
            qT = view(o, 4 * KB, BF16); o += 4 * KB
            kT = view(o, 4 * KB, BF16); o += 4 * KB
            vT = view(o, 4 * KB, BF16); o += 4 * KB
            vtok = []
            for b in range(3):
                vtok.append(view(o, 4 * KB, BF16).rearrange("p (j d) -> p j d", j=16)); o += 4 * KB
            num = view(o, 8 * KB, F32); o += 8 * KB
            lac = view(o, 8 * KB, F32); o += 8 * KB
            PT = [view(o + i * 512, 512, BF16) for i in range(4)]; o += 2 * KB
            rC = view(o, 8 * KB, F32); o += 8 * KB
            rS = view(o, 8 * KB, F32); o += 8 * KB
            t1 = view(o, 2 * KB, F32); o += 2 * KB
            t2 = view(o, 2 * KB, F32); o += 2 * KB
            oa = view(o, 4 * KB, BF16); o += 4 * KB
            assert o <= T_OFF + T_SIZE
            DMA("sp", rC[0:32, :], ropeA_d[0, 0:32, :], "rope", (), ["rope"])
            DMA("sp", rS[0:32, :], ropeA_d[1, 0:32, :], "rope", (), ["rope"])
            wv, wk = wload(w_evA[e][h], (16, 384))
            for ci, (dst, dk) in enumerate(((qT, "qT"), (kT, "kT"))):
                def ev(tb, bank, bkey, dst=dst, dk=dk):
                    ACT(dst[:, tb * TB:(tb + 1) * TB], bank, AF.Copy, [bkey], [dk])
                    return lambda: rope(dst, dk, tb, permA, 32, rC, rS, t1, t2)
                project(wv, wk, ci * 128, ev)

            def evv(tb, bank, bkey):
                ACT(vT[:, tb * TB:(tb + 1) * TB], bank, AF.Copy, [bkey], ["vT"])
            project(wv, wk, 256, evv)
            branches = (1, 4, 16)

            def tok_slice(dil, r, jb):
                s0 = dil * 128 * jb + r
                return slice(s0, s0 + dil * 127 + 1, dil)

            for b, dil in enumerate(branches):
                nbc = 16 // dil
                for g4 in range(4):
                    bank, bkey = nextbank()
                    for j in range(4):
                        bi = g4 * 4 + j
                        r, jb = bi // nbc, bi % nbc
                        MM(bank[:, j * 128:(j + 1) * 128], vT[:, tok_slice(dil, r, jb)], identb, True, True,
                           ["vT", "csb"], [bkey])
                    CP("dve", vtok[b][:, g4 * 4:g4 * 4 + 4, :], bank.rearrange("p (j d) -> p j d", j=4), [bkey], [("vtok", b)])
            scale = 128.0 ** -0.5
            cnt = 0
            blocks = []
            for b, dil in enumerate(branches):
                nbc = 16 // dil
                for g4 in range(4):
                    grp = b * 4 + g4
                    pob, pok = ps[6 + grp % 2][:], ("ps", 6 + grp % 2)
                    plb, plk = nextbank()
                    for j in range(4):
                        bi = g4 * 4 + j
                        r, jb = bi // nbc, bi % nbc
                        hp = jb > 0
                        sc_, pv_ = attn_block(cnt, kT[:, tok_slice(dil, r, jb)],
                                              kT[:, tok_slice(dil, r, jb - 1)] if hp else None,
                                              qT[:, tok_slice(dil, r, jb)],
                                              vtok[b][:, bi, :], vtok[b][:, bi - 1, :] if hp else None,
                                              maskA, scale, PT, pob[:, j * 128:(j + 1) * 128], plb[:, j * 128:(j + 1) * 128],
                                              (pok, plk), ["qT", "kT", ("vtok", b)], onesb[:], None)
                        cnt += 1
                        post = None
                        if j == 3:
                            def post(b=b, dil=dil, g4=g4, pob=pob, plb=plb, pok=pok, plk=plk):
                                if dil == 1:
                                    dn, dl_ = num[:, g4 * 512:(g4 + 1) * 512], lac[:, g4 * 512:(g4 + 1) * 512]
                                    sn, sl_ = pob, plb
                                elif dil == 4:
                                    dn, dl_ = num[:, g4::4], lac[:, g4::4]
                                    sn, sl_ = pob, plb
                                else:
                                    dn = num.rearrange("d (p r) -> d r p", r=16)[:, g4 * 4:g4 * 4 + 4, :]
                                    dl_ = lac.rearrange("d (p r) -> d r p", r=16)[:, g4 * 4:g4 * 4 + 4, :]
                                    sn = pob.rearrange("d (a p) -> d a p", a=4)
                                    sl_ = plb.rearrange("d (a p) -> d a p", a=4)
                                if b == 0:
                                    CP("dve", dn, sn, [pok], ["num"])
                                    ACT(dl_, sl_, AF.Copy, [plk], ["lac"])
                                else:
                                    TT("dve", dn, dn, sn, ALU.add, [pok, "num"], ["num"])
                                    TT("dve", dl_, dl_, sl_, ALU.add, [plk, "lac"], ["lac"])
                        blocks.append((sc_, pv_, post))
            attn_pipeline(blocks)
            RECIP(lac, lac, ["lac"], ["lac"])
            TT("dve", oa, num, lac, ALU.mult, ["num", "lac"], ["oa"])
            DMA("sp", MIXW[:, :, h, :], oa.rearrange("p (tb t) -> p tb t", tb=NTB), "mixst", ["oa"], [("MIXD", h)])

        def even_B(e, h):
            o = T_OFF
            def al(n, dt):
                nonlocal o
                v = view(o, n, dt)
                o += n
                return v
            qs = al(8 * KB, F32)
            gsil = al(4 * KB, BF16)
            g_reg = o
            gT = al(8 * KB, F32)
            Sb = view(g_reg, 8 * KB, BF16).rearrange("p (c v) -> p c v", c=32)
            kk_reg = o
            kkT = al(4 * KB, BF16)
            ob = view(kk_reg, 4 * KB, BF16)
            i_reg = o
            iT = al(4 * KB, BF16)
            khat = view(i_reg, 4 * KB, BF16).rearrange("p (j k) -> p j k", j=16)
            gtok = al(8 * KB, F32).rearrange("p (j k) -> p j k", j=16)
            kktok = al(4 * KB, BF16).rearrange("p (j k) -> p j k", j=16)
            itok = al(4 * KB, BF16).rearrange("p (j k) -> p j k", j=16)
            qtil = al(4 * KB, BF16)
            ktil = al(4 * KB, BF16)
            qd = al(4 * KB, BF16)
            tmp = [al(2 * KB, F32) for _ in range(4)]
            Sst = al(512, F32)
            Sst1 = al(512, F32)
            dec = al(128, F32)
            am = [al(512, BF16) for _ in range(2)]
            assert o <= T_OFF + T_SIZE, o - T_OFF
            KG, KK, KI = "B_g", "B_kk", "B_i"
            wv, wk = wload(w_evB[e][h], (16, 512))
            def ev_q(tb, bank, bkey):
                ACT(qs[:, tb * TB:(tb + 1) * TB], bank, AF.Silu, [bkey], ["qs"])
            project(wv, wk, 0, ev_q)
            def ev_g(tb, bank, bkey):
                ACT(gsil[:, tb * TB:(tb + 1) * TB], bank, AF.Silu, [bkey], ["gsil"])
            project(wv, wk, 128, ev_g)
            def ev_f(tb, bank, bkey):
                ACT(gT[:, tb * TB:(tb + 1) * TB], bank, AF.Sigmoid, [bkey], [KG])
            project(wv, wk, 256, ev_f)
            def ev_i(tb, bank, bkey):
                ACT(iT[:, tb * TB:(tb + 1) * TB], bank, AF.Copy, [bkey], [KI])
            project(wv, wk, 384, ev_i)
            bstop = cfg.get("bstop", 99)
            if bstop <= 0:
                return
            if e == 1:
                TS("dve", gT, gT, omlc[:, h:h + 1], lbc[:, h:h + 1], ALU.mult, ALU.add, [KG, "misc"], [KG])
            TS("dve", kkT, gT, -1.0, 1.0, ALU.mult, ALU.add, [KG], [KK])
            ACT(gT, gT, AF.Ln, [KG], [KG])
            if bstop <= 1:
                return
            for g4 in range(4):
                b1, k1 = nextbank()
                b2, k2 = nextbank()
                b3, k3 = nextbank()
                for j in range(4):
                    tt = g4 * 4 + j
                    ts_ = slice(tt * 128, (tt + 1) * 128)
                    MM(b1[:, j * 128:(j + 1) * 128], gT[:, ts_], identf, True, True, [KG, "cfs"], [k1])
                    MM(b2[:, j * 128:(j + 1) * 128], kkT[:, ts_], identb, True, True, [KK, "csb"], [k2])
                    MM(b3[:, j * 128:(j + 1) * 128], iT[:, ts_], identb, True, True, [KI, "csb"], [k3])
                r4 = lambda bk: bk.rearrange("p (j d) -> p j d", j=4)
                CP("dve", gtok[:, g4 * 4:g4 * 4 + 4, :], r4(b1), [k1], ["gtok"])
                ACT(kktok[:, g4 * 4:g4 * 4 + 4, :], r4(b2), AF.Copy, [k2], ["kktok"])
                CP("dve", itok[:, g4 * 4:g4 * 4 + 4, :], r4(b3), [k3], ["itok"])
            if bstop <= 2:
                return
            for g4 in range(4):
                blk = slice(g4 * 512, (g4 + 1) * 512)
                b1, k1 = nextbank()
                for j in range(4):
                    MM(b1[:, j * 128:(j + 1) * 128], gtok[:, g4 * 4 + j, :], trimid, True, True, ["gtok", "cfs"], [k1])
                ACT(tmp[0], b1, AF.Exp, [k1], ["tmp0"])
                ACT(tmp[1], b1, AF.Exp, [k1], ["tmp1"], scale=-1.0)
                TT("dve", qtil[:, blk], qs[:, blk], tmp[0], ALU.mult, ["qs", "tmp0"], ["qtil"])
                TT("dve", ktil[:, blk], kkT[:, blk], tmp[1], ALU.mult, [KK, "tmp1"], ["ktil"])
                b2, k2 = nextbank()
                for j in range(4):
                    MM(b2[:, j * 128:(j + 1) * 128], gtok[:, g4 * 4 + j, :], tristart, True, True, ["gtok", "cfs"], [k2])
                ACT(tmp[2], b2, AF.Exp, [k2], ["tmp2"])
                TT("dve", qd[:, blk], qs[:, blk], tmp[2], ALU.mult, ["qs", "tmp2"], ["qd"])
                CP("dve", dec[:, g4 * 8:(g4 + 1) * 8], tmp[2][:, 63::64], ["tmp2"], ["dec"])
            for g4 in range(4):
                b3, k3 = nextbank()
                for j in range(4):
                    MM(b3[:, j * 128:(j + 1) * 128], after, gtok[:, g4 * 4 + j, :], True, True, ["gtok", "cfs"], [k3])
                ACT(tmp[3], b3, AF.Exp, [k3], ["tmp3"])
                TT("dve", khat[:, g4 * 4:g4 * 4 + 4, :], kktok[:, g4 * 4:g4 * 4 + 4, :],
                   tmp[3].rearrange("p (j d) -> p j d", j=4), ALU.mult, ["kktok", "tmp3"], [KI])
            if bstop <= 3:
                return
            S2 = [Sst, Sst1]
            MEMSET("dve", S2[0], 0.0, [("Sst", 0)])
            MEMSET("dve", Sb[:, 0, :], 0.0, [KG])
            for g4 in range(4):
                bks = [nextbank(), nextbank()]
                for jj in range(4):
                    for par in range(2):
                        c = g4 * 8 + jj * 2 + par
                        tt, p0 = c // 2, 64 * par
                        MM(bks[par][0][:, jj * 128:(jj + 1) * 128], khat[p0:p0 + 64, tt, :], itok[p0:p0 + 64, tt, :],
                           True, True, [KI, "itok"], [bks[par][1]])
                for par in range(2):
                    ACT(tmp[par], bks[par][0], AF.Copy, [bks[par][1]], ["tmp%d" % par])
                for jj in range(4 if cfg.get("b4", 0) != 1 else 0):
                    for par in range(2):
                        c = g4 * 8 + jj * 2 + par
                        if c == 31:
                            break
                        STT("dve", S2[(c + 1) % 2], S2[c % 2], dec[:, c:c + 1], tmp[par][:, jj * 128:(jj + 1) * 128],
                            ALU.mult, ALU.add, [("Sst", c % 2), "dec", "tmp%d" % par], [("Sst", (c + 1) % 2)])
                        ACT(Sb[:, c + 1, :], S2[(c + 1) % 2], AF.Copy, [("Sst", (c + 1) % 2)], [KG])
            if bstop <= 4:
                return
            cK = 128.0 ** -0.5

            def stA(hf):
                g4, half = hf // 2, hf % 2
                sv, sk = pss[hf % 3]
                amv, amk = am[half], ("am", half)
                for j in range(4):
                    c = g4 * 8 + half * 4 + j
                    p0 = 64 * (c % 2)
                    cs = slice(c * 64, (c + 1) * 64)
                    MM(sv[p0:p0 + 64, (j // 2) * 64:(j // 2) * 64 + 64], ktil[:, cs], qtil[:, cs], True, True,
                       ["ktil", "qtil"], [sk])
                TT("dve", amv[:, 0:128], sv[:, 0:128], caus[:, 0:128], ALU.mult, [sk, "csb"], [amk])

            def stO(hf):
                g4, half = hf // 2, hf % 2
                amv, amk = am[half], ("am", half)
                for j in range(4):
                    c = g4 * 8 + half * 4 + j
                    tt, p0 = c // 2, 64 * (c % 2)
                    cs = slice(c * 64, (c + 1) * 64)
                    par = c % 2
                    oc = ((half * 4 + j) // 2) * 64
                    pb, pk = ps[6 + par], ("ps", 6 + par)
                    MM(pb[:, oc:oc + 64], itok[p0:p0 + 64, tt, :], amv[p0:p0 + 64, (j // 2) * 64:(j // 2) * 64 + 64],
                       True, False, ["itok", amk], [pk])
                    MM(pb[:, oc:oc + 64], Sb[:, c, :], qd[:, cs], False, True, [KG, "qd"], [pk])

            def stN1(g4):
                for par in range(2):
                    srcv = ps[6 + par][:, 0:256].rearrange("p (i t) -> p i t", t=64)
                    d0 = tmp[0].rearrange("p (i two t) -> p i two t", two=2, t=64)[:, :, par, :]
                    d1 = tmp[1].rearrange("p (i two t) -> p i two t", two=2, t=64)[:, :, par, :]
                    ACT(d0, srcv, AF.Square, [("ps", 6 + par)], ["tmp0"], scale=cK)
                    ACT(d1, srcv, AF.Copy, [("ps", 6 + par)], ["tmp1"], scale=cK)

            def stN2(g4):
                blk = slice(g4 * 512, (g4 + 1) * 512)
                bank, bkey = nextbank()
                MM(bank, onesf, tmp[0], True, True, ["tmp0", "misc"], [bkey])
                ACT(tmp[2], bank, AF.Sqrt, [bkey, "misc"], ["tmp2"], bias=epsc, scale=1.0 / 128.0)
                RECIP(tmp[2], tmp[2], ["tmp2"], ["tmp2"])
                STT("dve", tmp[3], tmp[1], sms[:, SM_GNORM + e:SM_GNORM + e + 1], tmp[2], ALU.mult, ALU.mult,
                    ["tmp1", "tmp2", "sms"], ["tmp3"])
                TT("dve", ob[:, blk], tmp[3], gsil[:, blk], ALU.mult, ["tmp3", "gsil"], [KK])

            stA(0)
            stA(1)
            stO(0)
            for g4 in range(4):
                if g4 < 3:
                    stA(2 * g4 + 2)
                stO(2 * g4 + 1)
                stN1(g4)
                if g4 < 3:
                    stA(2 * g4 + 3)
                    stO(2 * g4 + 2)
                stN2(g4)
            DMA("sp", MIXW[:, :, 8 + h, :], ob.rearrange("p (tb t) -> p tb t", tb=NTB), "mixst", [KK], [("MIXD", 8 + h)])

        def odd_group(oi, g):
            o = T_OFF
            def al(n, dt):
                nonlocal o
                v = view(o, n, dt)
                o += n
                return v
            qc = [al(4 * KB, BF16) for _ in range(4)]
            kd = al(4 * KB, BF16)
            vT = al(4 * KB, BF16)
            vtok = al(2 * KB, BF16).rearrange("p (j d) -> p j d", j=16)
            rC = al(8 * KB, F32)
            rS = al(8 * KB, F32)
            t1 = al(2 * KB, F32)
            t2 = al(2 * KB, F32)
            ev = [al(2 * KB, F32) for _ in range(3)]
            PT = [al(512, BF16) for _ in range(4)]
            assert o <= T_OFF + T_SIZE
            if g == 0:
                DMA("sp", rC, ropeC_d[0], "rope", (), ["rope"])
                DMA("sp", rS, ropeC_d[1], "rope", (), ["rope"])
            wq, wqk = wload(w_odQ[oi][g], (16, 512))
            wkv, wkvk = wload(w_odKV[oi][g], (16, 192))
            for ci in range(4):
                c = g * 4 + ci
                def evq(tb, bank, bkey, ci=ci, c=c):
                    ACT(qc[ci][:, tb * TB:(tb + 1) * TB], bank, AF.Identity, [bkey, "sms"], [("qc", ci, tb)],
                        bias=sms[:, SM_BQ + oi * 16 + c:SM_BQ + oi * 16 + c + 1])
                    return lambda: rope(qc[ci], ("qc", ci, tb), tb, permC, 128, rC, rS, t1, t2)
                project(wq, wqk, ci * 128, evq)
            def evk(tb, bank, bkey):
                ACT(kd[:, tb * TB:(tb + 1) * TB], bank, AF.Identity, [bkey, "sms"], ["kd"],
                    bias=sms[:, SM_BK + oi * 4 + g:SM_BK + oi * 4 + g + 1])
                return lambda: rope(kd, "kd", tb, permC, 128, rC, rS, t1, t2)
            project(wkv, wkvk, 0, evk)
            for tb in range(NTB):
                bank, bkey = nextbank()
                for kc in range(NKC):
                    MM(bank[0:64, :], wkv[:, kc, 128:192], HB[:, kc, tb * TB:(tb + 1) * TB], kc == 0, kc == NKC - 1,
                       [wkvk, ("HB", tb)], [bkey])
                ACT(vT[0:64, tb * TB:(tb + 1) * TB], bank[0:64, :], AF.Copy, [bkey], ["vT"])
            for g4 in range(4):
                bank, bkey = nextbank()
                for j in range(4):
                    tt = g4 * 4 + j
                    MM(bank[:, j * 64:(j + 1) * 64], vT[0:64, tt * 128:(tt + 1) * 128], identb[0:64, 0:64], True, True,
                       ["vT", "csb"], [bkey])
                CP("dve", vtok[:, g4 * 4:g4 * 4 + 4, :], bank[:, 0:256].rearrange("p (j d) -> p j d", j=4), [bkey], ["vtok"])
            scale = 64.0 ** -0.5
            cnt = 0
            blocks = []
            for ci in range(4):
                c = g * 4 + ci
                for g4 in range(4):
                    grp = ci * 4 + g4
                    pob, pok = ps[6 + grp % 2][:], ("ps", 6 + grp % 2)
                    plb, plk = nextbank()
                    for hh in range(2):
                        p0 = 64 * hh
                        for j in range(4):
                            qb = g4 * 4 + j
                            hp = qb > 0
                            cur = slice(qb * 128, (qb + 1) * 128)
                            prv = slice((qb - 1) * 128, qb * 128)
                            sc_, pv_ = attn_block(cnt, kd[p0:p0 + 64, cur], kd[p0:p0 + 64, prv] if hp else None,
                                                  qc[ci][p0:p0 + 64, cur], vtok[:, qb, :], vtok[:, qb - 1, :] if hp else None,
                                                  maskC, scale, PT, pob[p0:p0 + 64, j * 128:(j + 1) * 128],
                                                  plb[p0:p0 + 64, j * 128:(j + 1) * 128], (pok, plk),
                                                  [("qc", ci, g4), "kd", "vtok"], onesb[:, 0:64], None)
                            cnt += 1
                            post = None
                            if hh == 1 and j == 3:
                                def post(ci=ci, c=c, g4=g4, pob=pob, plb=plb, pok=pok, plk=plk):
                                    blk = slice(g4 * 512, (g4 + 1) * 512)
                                    bvc = sms[:, SM_BV + oi * 4 + g:SM_BV + oi * 4 + g + 1]
                                    esc = esink[:, oi * 16 + c:oi * 16 + c + 1]
                                    ACT(ev[0], plb, AF.Identity, [plk, "misc"], ["ev0"], bias=esc)
                                    RECIP(ev[0], ev[0], ["ev0"], ["ev0"])
                                    ACT(ev[1], plb, AF.Copy, [plk, "sms"], ["ev1"], scale=bvc)
                                    TT("dve", ev[2], ev[1], pob, ALU.add, ["ev1", pok], ["ev2"])
                                    TT("dve", qc[ci][:, blk], ev[2], ev[0], ALU.mult, ["ev2", "ev0"], [("qc", ci, g4)])
                                    if g4 == 3:
                                        DMA("sp", MIXW[:, :, c, :], qc[ci].rearrange("p (tb t) -> p tb t", tb=NTB), "mixst",
                                            [("qc", ci, t_) for t_ in range(4)], [("MIXD", c)])
                            blocks.append((sc_, pv_, post))
            attn_pipeline(blocks)

        store_ops = []

        def phase_out(li, spec, last):
            l = spec["l"]
            mixer = spec["mixer"]
            hid = view(HB_OFF, 32 * KB, BF16).rearrange("p (f t) -> p f t", f=32)
            xb = view(HB_OFF + 32 * KB, 32 * KB, F32).rearrange("p (kc t) -> p kc t", kc=NKC)
            mxs = [view(T_OFF, 16 * KB, BF16).rearrange("p (kc t) -> p kc t", kc=NKC),
                   view(T_OFF + 44 * KB, 16 * KB, BF16).rearrange("p (kc t) -> p kc t", kc=NKC)]
            hbb = view(T_OFF + 16 * KB, 16 * KB, BF16).rearrange("p (kc t) -> p kc t", kc=NKC)
            o = T_OFF + 32 * KB
            tv = {"sq": [view(o + i * 2 * KB, 2 * KB, F32) for i in range(2)],
                  "rt": view(o + 4 * KB, 2 * KB, F32), "rr": view(o + 6 * KB, 2 * KB, F32)}
            rl = [view(o + 8 * KB + i * 2 * KB, 2 * KB, F32) for i in range(2)]
            src = xT if li == 0 else XS
            dst = outT if (last and not final) else XS
            xbf = xb.rearrange("p kc t -> p (kc t)")
            xk = lambda m: ("xb", m)
            del store_ops[:]

            def load_x(tb, m):
                DMA("sp", xb[:, m, :], src[tb][:, m * TB:(m + 1) * TB], ("xl", m), [("XS", tb)], [xk(m)])

            def load_mx(tb):
                DMA("sp", mxs[tb % 2].rearrange("p kc t -> p (kc t)"), MIXD[tb].rearrange("p c t -> p (c t)"), ("mxl", tb % 2),
                    [("MIXD", c) for c in range(16)], [("mx", tb % 2)])

            def store_x(tb, m):
                wkey = [("XS", tb)] if dst is XS else [("OUT", tb)]
                store_ops.append(DMA("sp", dst[tb][:, m * TB:(m + 1) * TB], xb[:, m, :], ("xs", m), [xk(m)], wkey))
                if tb + 1 < NTB:
                    load_x(tb + 1, m)

            for m in range(NKC):
                load_x(0, m)
            if mixer is not None:
                load_mx(0)
            for tb in range(NTB):
                mx, mxk = mxs[tb % 2], ("mx", tb % 2)
                streamed = not (last and final)
                if mixer is not None:
                    wsrc = w_evO[l // 2] if mixer == "even" else w_odO[l // 2]
                    for mg in range(4):
                        wv, wk = wload(wsrc[mg], (16, 512))
                        for mi in range(4):
                            m = mg * 4 + mi
                            bank, bkey = nextbank()
                            for kc in range(NKC):
                                MM(bank, wv[:, kc, mi * 128:(mi + 1) * 128], mx[:, kc, :], kc == 0, kc == NKC - 1,
                                   [wk, mxk], [bkey])
                            if mixer == "odd":
                                r_, rk = rl[m % 2], ("rl", m % 2)
                                ACT(r_, bank, AF.Identity, [bkey, "sms"], [rk],
                                    bias=sms[:, SM_BO + (l // 2) * 16 + m:SM_BO + (l // 2) * 16 + m + 1])
                                TT("dve", xb[:, m, :], r_, xb[:, m, :], ALU.add, [rk, xk(m)], [xk(m)])
                            else:
                                TT("dve", xb[:, m, :], bank, xb[:, m, :], ALU.add, [bkey, xk(m)], [xk(m)])
                            if streamed and not spec["mlp"]:
                                store_x(tb, m)
                    if tb + 1 < NTB:
                        load_mx(tb + 1)
                elif streamed and not spec["mlp"]:
                    for m in range(NKC):
                        store_x(tb, m)
                if spec["mlp"]:
                    norm_block(xb, xk, (4 + l) * 16, hbb, "hbb", tv)
                    for half in range(2):
                        for fg in range(8):
                            wv, wk = wload(w1p[l][half * 8 + fg], (16, 512))
                            for fi in range(4):
                                f = fg * 4 + fi
                                bank, bkey = nextbank()
                                for kc in range(NKC):
                                    MM(bank, wv[:, kc, fi * 128:(fi + 1) * 128], hbb[:, kc, :], kc == 0, kc == NKC - 1,
                                       [wk, "hbb"], [bkey])
                                r_, rk = rl[f % 2], ("rl", f % 2)
                                ACT(r_, bank, AF.Relu, [bkey], [rk])
                                TT("dve", hid[:, f, :], r_, r_, ALU.mult, [rk], [("hid", f)])
                        for mp in range(8):
                            wv, wk = wload(w2p[l][half, mp], (32, 256))
                            for mi in range(2):
                                m = mp * 2 + mi
                                bank, bkey = nextbank()
                                for f in range(32):
                                    MM(bank, wv[:, f, mi * 128:(mi + 1) * 128], hid[:, f, :], f == 0, f == 31,
                                       [wk, ("hid", f)], [bkey])
                                TT("dve", xb[:, m, :], bank, xb[:, m, :], ALU.add, [bkey, xk(m)], [xk(m)])
                                if streamed and half == 1:
                                    store_x(tb, m)
                if not streamed:
                    norm_block(xb, xk, 8 * 16, xb, xk, tv)
                    store_ops.append(DMA("sp", outT[tb], xbf, "xst", [xk(m) for m in range(NKC)], [("OUT", tb)]))
                    if tb + 1 < NTB:
                        for m in range(NKC):
                            load_x(tb + 1, m)

        for li, spec in enumerate(layers):
            l = spec["l"]
            last = li == len(layers) - 1
            if spec["mixer"] is not None:
                phase_norm(li, l)
                S.barrier()
                if spec["mixer"] == "even":
                    for h in spec.get("heads", range(8)):
                        if spec.get("A", True):
                            even_A(l // 2, h)
                            S.barrier()
                        if spec.get("B", True):
                            even_B(l // 2, h)
                            S.barrier()
                else:
                    for g in range(4):
                        odd_group(l // 2, g)
                        S.barrier()
            phase_out(li, spec, last)
            S.barrier()
        S.emit(list(store_ops))
    return nc


def prep_shared(inp):
    sh = {}
    c = _host_consts()
    sh.update(c)
    g = np.concatenate([inp["norm_mix_g"], inp["norm_mlp_g"], inp["final_norm_g"][None, :]], axis=0)
    sh["gains"] = np.ascontiguousarray(g.reshape(9, 16, 128).transpose(2, 0, 1).reshape(128, 144))
    wi = inp["even_w_in"]
    def colsA(h):
        return np.r_[h * 128:(h + 1) * 128, 1024 + h * 128:1024 + (h + 1) * 128, 2048 + h * 128:2048 + (h + 1) * 128]
    def colsB(h):
        return np.r_[3072 + h * 128:3072 + (h + 1) * 128, 6144 + h * 128:6144 + (h + 1) * 128,
                     4096 + h * 128:4096 + (h + 1) * 128, 5120 + h * 128:5120 + (h + 1) * 128]
    A = np.stack([np.stack([wi[e][:, colsA(h)] for h in range(8)]) for e in range(2)])
    sh["w_evA"] = np.ascontiguousarray(A.reshape(2, 8, 16, 128, 384).transpose(0, 1, 3, 2, 4))
    B = np.stack([np.stack([wi[e][:, colsB(h)] for h in range(8)]) for e in range(2)])
    sh["w_evB"] = np.ascontiguousarray(B.reshape(2, 8, 16, 128, 512).transpose(0, 1, 3, 2, 4))
    def slab512(w):
        n = w.shape[0]
        return np.ascontiguousarray(w.reshape(n, 16, 128, 4, 512).transpose(0, 3, 2, 1, 4))
    sh["w_evO"] = slab512(inp["even_w_out"])
    sh["w_odO"] = slab512(inp["odd_w_o"])
    wq = inp["odd_w_qkv"]
    sh["w_odQ"] = slab512(wq[:, :, 0:2048])
    kv = np.stack([np.stack([np.concatenate([wq[o][:, 2048 + g * 64:2048 + (g + 1) * 64],
                                             wq[o][:, 2048 + g * 64:2048 + (g + 1) * 64],
                                             wq[o][:, 2304 + g * 64:2304 + (g + 1) * 64]], axis=1) for g in range(4)])
                   for o in range(2)])
    sh["w_odKV"] = np.ascontiguousarray(kv.reshape(2, 4, 16, 128, 192).transpose(0, 1, 3, 2, 4))
    w1 = inp["mlp_w1"]
    sh["w1p"] = np.ascontiguousarray(w1.reshape(4, 16, 128, 16, 512).transpose(0, 3, 2, 1, 4))
    w2 = inp["mlp_w2"]
    sh["w2p"] = np.ascontiguousarray(w2.reshape(4, 2, 32, 128, 8, 256).transpose(0, 1, 4, 3, 2, 5))
    sm = np.zeros((128, 256), np.float32)
    lb = inp["hgrn_lb_raw"].reshape(2, 8, 128)
    sm[:, 0:16] = lb.transpose(2, 0, 1).reshape(128, 16)
    sm[:, 16:18] = inp["hgrn_norm_g"].T
    bq = inp["odd_b_qkv"]
    sm[:, 18:50] = bq[:, 0:2048].reshape(2, 16, 128).transpose(2, 0, 1).reshape(128, 32)
    bk = bq[:, 2048:2304].reshape(2, 4, 64)
    sm[:, 50:58] = np.concatenate([bk, bk], axis=2).transpose(2, 0, 1).reshape(128, 8)
    bv = bq[:, 2304:2560].reshape(2, 4, 64)
    sm[:, 58:66] = np.concatenate([bv, bv], axis=2).transpose(2, 0, 1).reshape(128, 8)
    sk = inp["odd_sinks"].reshape(2, 16, 2)
    sm[:, 66:98] = np.repeat(sk, 64, axis=2).transpose(2, 0, 1).reshape(128, 32)
    sm[:, 98:130] = inp["odd_b_o"].reshape(2, 16, 128).transpose(2, 0, 1).reshape(128, 32)
    sh["smalls"] = sm
    return sh


FULL_CFG = {"layers": [{"l": 0, "mixer": "even", "mlp": True}, {"l": 1, "mixer": "odd", "mlp": True},
                       {"l": 2, "mixer": "even", "mlp": True}, {"l": 3, "mixer": "odd", "mlp": True}], "final": True}


def run(inputs, cfg, ncores=8):
    inp = {k: np.asarray(v) for k, v in inputs.items()}
    sh = prep_shared(inp)
    nc = build(cfg)
    names = set()
    for alloc in nc.allocations:
        try:
            if alloc.kind == "ExternalInput":
                names.add(alloc.memorylocations[0].name)
        except Exception:
            pass
    shared = {}
    for k, v in sh.items():
        if k in names:
            shared[k] = v
        elif k.startswith("w"):
            for i in range(v.shape[0]):
                if (k + str(i)) in names:
                    shared[k + str(i)] = np.ascontiguousarray(v[i])
    x = inp["x"]
    in_maps = []
    for b in range(ncores):
        m = dict(shared)
        m["xT"] = np.ascontiguousarray(x[b].T.reshape(NKC, 128, NTB, TB).transpose(2, 1, 0, 3)).reshape(NTB, 128, NKC * TB)
        in_maps.append(m)
    res = run_bass_kernel_spmd(nc, in_maps, core_ids=list(range(ncores)))
    outs = []
    for r in res.results:
        o = np.asarray(r["outT"]).reshape(NTB, 128, NKC, TB).transpose(2, 1, 0, 3).reshape(D, SEQ)
        outs.append(np.ascontiguousarray(o.T))
    return np.stack(outs).astype(np.float32)


def kernel(**inputs):
    return run(inputs, FULL_CFG, 8)
```
